# Optimizing a Trainium2 kernel written in Bass

```python
import math
import jax, jax.numpy as jnp
from jax import lax
import numpy as np

D_MODEL = 2048
BATCH = 1
SEQ = 16384
DEPTH = 4

GRID_W = 64
CTX_LEN = 256
Q_BLOCK = 128
ROPE_THETA = 10000.0
EPS = 1e-6

N_BRANCH = 3
BRANCH_W = D_MODEL // 2
RET_HEADS = 8
RET_KEY_DIM = BRANCH_W // (2 * RET_HEADS)
RET_VAL_DIM = BRANCH_W // RET_HEADS
RET_CHUNK = 128
DIFF_HEADS = 8
DIFF_HEAD_DIM = BRANCH_W // (2 * DIFF_HEADS)
DIFF_VAL_DIM = 2 * DIFF_HEAD_DIM
GQA_HEAD_DIM = 128
GQA_Q_HEADS = BRANCH_W // GQA_HEAD_DIM
GQA_KV_HEADS = 2
GQA_GROUP = GQA_Q_HEADS // GQA_KV_HEADS

RET_QK_W = RET_HEADS * RET_KEY_DIM
RET_V_W = RET_HEADS * RET_VAL_DIM
DIFF_QK_W = DIFF_HEADS * 2 * DIFF_HEAD_DIM
DIFF_V_W = DIFF_HEADS * DIFF_VAL_DIM
GQA_Q_W = GQA_Q_HEADS * GQA_HEAD_DIM
GQA_KV_W = GQA_KV_HEADS * GQA_HEAD_DIM
IN_SIZES = (RET_QK_W, RET_QK_W, RET_V_W, BRANCH_W,
            DIFF_QK_W, DIFF_QK_W, DIFF_V_W, BRANCH_W,
            GQA_Q_W, GQA_KV_W, GQA_KV_W, BRANCH_W,
            N_BRANCH * D_MODEL)
IN_COLS = sum(IN_SIZES)

kernel_name = "hybrid_parallel_retention_diffattn_gqa_dit"

F32 = jnp.float32


def rms_norm(x, gain=None):
    xf = x.astype(F32)
    y = xf * lax.rsqrt(jnp.mean(xf * xf, axis=-1, keepdims=True) + EPS)
    if gain is not None:
        y = y * gain.astype(F32)
    return y.astype(x.dtype)


def split_proj(z):
    parts, start = [], 0
    for size in IN_SIZES:
        parts.append(z[..., start:start + size])
        start += size
    return parts


def heads(z, n):
    b, t, _ = z.shape
    return z.reshape(b, t, n, -1).transpose(0, 2, 1, 3)


def rope_rotate(x, cos, sin):
    half = x.shape[-1] // 2
    x1, x2 = x[..., :half], x[..., half:]
    cos = cos.astype(x.dtype)
    sin = sin.astype(x.dtype)
    return jnp.concatenate([x1 * cos - x2 * sin, x1 * sin + x2 * cos], axis=-1)


def axial_rope_tables(n_tokens, head_dim):
    n_rows = n_tokens // GRID_W
    rows = jnp.repeat(jnp.arange(n_rows), GRID_W).astype(F32)
    cols = jnp.tile(jnp.arange(GRID_W), n_rows).astype(F32)
    n_freq = head_dim // 4
    freqs = ROPE_THETA ** (-jnp.arange(n_freq, dtype=F32) / n_freq)
    ang = jnp.concatenate([rows[:, None] * freqs, cols[:, None] * freqs], axis=-1)
    return jnp.cos(ang), jnp.sin(ang)


def seq_rope_tables(n_tokens, head_dim):
    freqs = 1.0 / (ROPE_THETA ** jnp.linspace(0.0, 1.0, head_dim // 2, dtype=F32))
    ang = jnp.arange(n_tokens, dtype=F32)[:, None] * freqs
    return jnp.cos(ang), jnp.sin(ang)


def ret_states(k, v, log_g, s0):
    b, h, t, dk = k.shape
    n = t // RET_CHUNK
    kc = k.reshape(b, h, n, RET_CHUNK, dk)
    vc = v.reshape(b, h, n, RET_CHUNK, -1)
    j = jnp.arange(RET_CHUNK)
    w = jnp.exp(log_g[:, None] * (RET_CHUNK - 1 - j))
    kv = jnp.einsum('bhncd,bhnce,hc->nbhde', kc, vc, w)
    g_chunk = jnp.exp(log_g * RET_CHUNK)[None, :, None, None]

    def step(s, kv_n):
        return g_chunk * s + kv_n, s

    s_final, s_prev = lax.scan(step, s0, kv)
    return s_prev, s_final


def ret_output(q, k, v, log_g, s_prev, strict):
    b, h, t, dk = q.shape
    n = t // RET_CHUNK
    qc = q.reshape(b, h, n, RET_CHUNK, dk)
    kc = k.reshape(b, h, n, RET_CHUNK, dk)
    vc = v.reshape(b, h, n, RET_CHUNK, -1)
    i = jnp.arange(RET_CHUNK)
    rel = i[:, None] - i[None, :]
    mask = (rel > 0) if strict else (rel >= 0)
    decay = jnp.where(mask[None], jnp.exp(log_g[:, None, None] * jnp.where(mask, rel, 0)[None]), 0.0)
    att = jnp.einsum('bhnid,bhnjd->bhnij', qc, kc) * decay[:, None]
    o = jnp.einsum('bhnij,bhnje->bhnie', att, vc)
    cross = jnp.exp(log_g[:, None] * (i + 1))
    o = o + jnp.einsum('bhnid,nbhde->bhnie', qc * cross[:, None, :, None], s_prev)
    return o.reshape(b, h, t, -1)


def retention_finish(o, g):
    b, h, t, dv = o.shape
    o = rms_norm(o).astype(g.dtype).transpose(0, 2, 1, 3).reshape(b, t, h * dv)
    return o * jax.nn.silu(g)


def retention_branch(zl, zc, log_rate, with_ctx_out):
    ql, kl, vl, gl = zl
    qc, kc, vc, gc = zc
    b, t, _ = ql.shape
    k_scale = RET_KEY_DIM ** -0.5
    cos, sin = seq_rope_tables(t, RET_KEY_DIM)
    ql = rope_rotate(heads(ql, RET_HEADS), cos, sin)
    kl = rope_rotate(heads(kl, RET_HEADS), cos, sin) * k_scale
    vl = heads(vl, RET_HEADS)
    qc = heads(qc, RET_HEADS)
    kc = heads(kc, RET_HEADS) * k_scale
    vc = heads(vc, RET_HEADS)
    log_g = -jnp.exp(log_rate.astype(F32))
    s0 = jnp.zeros((b, RET_HEADS, RET_KEY_DIM, RET_VAL_DIM), F32)
    flip = lambda a: jnp.flip(a, axis=2)
    pc_f, sc_f = ret_states(kc, vc, log_g[0], s0)
    pc_b, sc_b = ret_states(flip(kc), flip(vc), log_g[1], s0)
    pl_f, _ = ret_states(kl, vl, log_g[0], sc_f)
    pl_b, _ = ret_states(flip(kl), flip(vl), log_g[1], sc_b)
    ol = (ret_output(ql, kl, vl, log_g[0], pl_f, False)
          + flip(ret_output(flip(ql), flip(kl), flip(vl), log_g[1], pl_b, True)))
    out_l = retention_finish(ol, gl)
    out_c = None
    if with_ctx_out:
        oc = (ret_output(qc, kc, vc, log_g[0], pc_f, False)
              + flip(ret_output(flip(qc), flip(kc), flip(vc), log_g[1], pc_b, True)))
        out_c = retention_finish(oc, gc)
    return out_l, out_c


def diff_attend(q, k, v, lam):
    b, h, _, t, d = q.shape
    nb = t // Q_BLOCK
    qb = jnp.moveaxis(q.reshape(b, h, 2, nb, Q_BLOCK, d), 3, 0)
    scale = d ** -0.5

    def block(qi):
        s = jnp.einsum('bhcqd,bhcsd->bhcqs', qi, k).astype(F32) * scale
        p = jax.nn.softmax(s, axis=-1)
        pd = p[:, :, 0] - lam * p[:, :, 1]
        return jnp.einsum('bhqs,bhse->bhqe', pd.astype(v.dtype), v)

    o = lax.map(block, qb)
    return jnp.moveaxis(o, 0, 2).reshape(b, h, t, -1)


def diff_finish(o, g, subln_gain, lambda_init):
    b, h, t, dv = o.shape
    o = rms_norm(o, subln_gain) * (1.0 - lambda_init)
    o = o.transpose(0, 2, 1, 3).reshape(b, t, h * dv)
    return o * jax.nn.silu(g)


def diff_branch(zl, zc, lam_params, subln_gain, lambda_init, with_ctx_out):
    ql, kl, vl, gl = zl
    qc, kc, vc, gc = zc

    def qk_heads(z):
        bb, tt, _ = z.shape
        return z.reshape(bb, tt, DIFF_HEADS, 2, DIFF_HEAD_DIM).transpose(0, 2, 3, 1, 4)

    t = ql.shape[1]
    cos, sin = axial_rope_tables(t, DIFF_HEAD_DIM)
    ql = rope_rotate(qk_heads(ql), cos, sin)
    kl = rope_rotate(qk_heads(kl), cos, sin)
    vl = heads(vl, DIFF_HEADS)
    qc, kc, vc = qk_heads(qc), qk_heads(kc), heads(vc, DIFF_HEADS)
    lp = lam_params.astype(F32)
    lam = jnp.exp(jnp.sum(lp[0] * lp[1])) - jnp.exp(jnp.sum(lp[2] * lp[3])) + lambda_init
    k_all = jnp.concatenate([kl, kc], axis=3)
    v_all = jnp.concatenate([vl, vc], axis=2)
    out_l = diff_finish(diff_attend(ql, k_all, v_all, lam), gl, subln_gain, lambda_init)
    out_c = None
    if with_ctx_out:
        out_c = diff_finish(diff_attend(qc, kc, vc, lam), gc, subln_gain, lambda_init)
    return out_l, out_c


def gqa_attend(q, k, v):
    b, kh, g, t, d = q.shape
    nb = t // Q_BLOCK
    qb = jnp.moveaxis(q.reshape(b, kh, g, nb, Q_BLOCK, d), 3, 0)
    scale = d ** -0.5

    def block(qi):
        s = jnp.einsum('bkgqd,bksd->bkgqs', qi, k).astype(F32) * scale
        p = jax.nn.softmax(s, axis=-1).astype(v.dtype)
        return jnp.einsum('bkgqs,bkse->bkgqe', p, v)

    o = lax.map(block, qb)
    return jnp.moveaxis(o, 0, 3).reshape(b, kh, g, t, -1)


def gqa_finish(o, g):
    b, kh, gg, t, d = o.shape
    o = o.transpose(0, 3, 1, 2, 4).reshape(b, t, kh * gg * d)
    return o * jax.nn.silu(g)


def gqa_branch(zl, zc, q_gain, k_gain, with_ctx_out):
    ql, kl, vl, gl = zl
    qc, kc, vc, gc = zc

    def q_heads(z):
        bb, tt, _ = z.shape
        return z.reshape(bb, tt, GQA_KV_HEADS, GQA_GROUP, GQA_HEAD_DIM).transpose(0, 2, 3, 1, 4)

    t = ql.shape[1]
    cos, sin = axial_rope_tables(t, GQA_HEAD_DIM)
    ql = rope_rotate(rms_norm(q_heads(ql), q_gain), cos, sin)
    kl = rope_rotate(rms_norm(heads(kl, GQA_KV_HEADS), k_gain), cos, sin)
    vl = heads(vl, GQA_KV_HEADS)
    qc = rms_norm(q_heads(qc), q_gain)
    kc = rms_norm(heads(kc, GQA_KV_HEADS), k_gain)
    vc = heads(vc, GQA_KV_HEADS)
    k_all = jnp.concatenate([kl, kc], axis=2)
    v_all = jnp.concatenate([vl, vc], axis=2)
    out_l = gqa_finish(gqa_attend(ql, k_all, v_all), gl)
    out_c = None
    if with_ctx_out:
        out_c = gqa_finish(gqa_attend(qc, kc, vc), gc)
    return out_l, out_c


def merge_branches(branches, z_gate, w_branch, w_out):
    br = jnp.stack(branches, axis=2)
    y = jnp.einsum('btnw,nwd->btnd', br, w_branch)
    gates = jax.nn.sigmoid(z_gate.reshape(z_gate.shape[:-1] + (N_BRANCH, D_MODEL)))
    return jnp.einsum('btnd,btnd->btd', gates, y) @ w_out


def mixer(h_lat, h_ctx, w_in, ret_log_rate, diff_lambda, diff_subln_gain, gqa_q_gain,
          gqa_k_gain, w_branch, w_out, lambda_init, with_ctx_out):
    zl = split_proj(h_lat @ w_in)
    zc = split_proj(h_ctx @ w_in)
    ra, rc = retention_branch(zl[0:4], zc[0:4], ret_log_rate, with_ctx_out)
    da, dc = diff_branch(zl[4:8], zc[4:8], diff_lambda, diff_subln_gain, lambda_init, with_ctx_out)
    ga, gc = gqa_branch(zl[8:12], zc[8:12], gqa_q_gain, gqa_k_gain, with_ctx_out)
    out_l = merge_branches([ra, da, ga], zl[12], w_branch, w_out)
    out_c = merge_branches([rc, dc, gc], zc[12], w_branch, w_out) if with_ctx_out else None
    return out_l, out_c


def setup_inputs(seed: int = 0) -> dict:
    key = jax.random.key(seed)
    ks = jax.random.split(key, 16)
    nrm = jax.random.normal
    ret_base = jnp.log(-jnp.log1p(-(2.0 ** (-5.0 - jnp.arange(RET_HEADS, dtype=F32)))))
    return {
        "x": nrm(ks[0], (BATCH, SEQ, D_MODEL), F32),
        "c": nrm(ks[1], (BATCH, D_MODEL), F32),
        "ctx": nrm(ks[2], (BATCH, CTX_LEN, D_MODEL), F32),
        "c_ctx": nrm(ks[3], (D_MODEL,), F32),
        "norm_gain": 1.0 + 0.02 * nrm(ks[4], (DEPTH, D_MODEL), F32),
        "w_ada": nrm(ks[5], (DEPTH, D_MODEL, 3 * D_MODEL), F32) * (0.5 * D_MODEL ** -0.5),
        "b_ada": 0.01 * nrm(ks[6], (DEPTH, 3 * D_MODEL), F32),
        "w_in": nrm(ks[7], (DEPTH, D_MODEL, IN_COLS), F32) * D_MODEL ** -0.5,
        "ret_log_rate": ret_base[None, None, :] + 0.05 * nrm(ks[8], (DEPTH, 2, RET_HEADS), F32),
        "diff_lambda": 0.1 * nrm(ks[9], (DEPTH, 4, DIFF_HEAD_DIM), F32),
        "diff_subln_gain": 1.0 + 0.02 * nrm(ks[10], (DEPTH, DIFF_VAL_DIM), F32),
        "gqa_q_gain": 1.0 + 0.02 * nrm(ks[11], (DEPTH, GQA_HEAD_DIM), F32),
        "gqa_k_gain": 1.0 + 0.02 * nrm(ks[12], (DEPTH, GQA_HEAD_DIM), F32),
        "w_branch": nrm(ks[13], (DEPTH, N_BRANCH, BRANCH_W, D_MODEL), F32) * BRANCH_W ** -0.5,
        "w_out": nrm(ks[14], (DEPTH, D_MODEL, D_MODEL), F32) * D_MODEL ** -0.5,
        "final_norm_gain": 1.0 + 0.02 * nrm(ks[15], (D_MODEL,), F32),
    }


def reference(x, c, ctx, c_ctx, norm_gain, w_ada, b_ada, w_in, ret_log_rate, diff_lambda,
              diff_subln_gain, gqa_q_gain, gqa_k_gain, w_branch, w_out, final_norm_gain):
    sc = jax.nn.silu(c)
    scc = jax.nn.silu(c_ctx)
    for l in range(DEPTH):
        last = l == DEPTH - 1
        shift, scale, gate = jnp.split(sc @ w_ada[l] + b_ada[l], 3, axis=-1)
        shift_c, scale_c, gate_c = jnp.split(scc @ w_ada[l] + b_ada[l], 3, axis=-1)
        h = rms_norm(x, norm_gain[l]) * (1.0 + scale[:, None]) + shift[:, None]
        hc = rms_norm(ctx, norm_gain[l]) * (1.0 + scale_c) + shift_c
        lambda_init = 0.8 - 0.6 * math.exp(-0.3 * l)
        out, out_c = mixer(h, hc, w_in[l], ret_log_rate[l], diff_lambda[l], diff_subln_gain[l],
                           gqa_q_gain[l], gqa_k_gain[l], w_branch[l], w_out[l], lambda_init,
                           not last)
        x = x + gate[:, None] * out
        if not last:
            ctx = ctx + gate_c * out_c
    return rms_norm(x, final_norm_gain)
```

```python
import os
import numpy as np
from contextlib import ExitStack
import concourse.bass as bass
import concourse.mybir as mybir
from concourse.bass_utils import run_bass_kernel_spmd

F32 = mybir.dt.float32
BF16 = mybir.dt.bfloat16
AF = mybir.ActivationFunctionType
ALU = mybir.AluOpType
AX = mybir.AxisListType

NCORE = 8
D = 2048
KC = 16
CTX = 256
CL = 32
GRID_W = 64
EPS = 1e-6
IN_COLS = 15872
NTF = 98
STOP = int(os.environ.get('KSTOP', '99'))


def _w_in_perm():
    T = list(range(0, 512)) + list(range(512, 1024)) + list(range(1024, 2048)) \
        + list(range(5120, 6144)) + list(range(8448, 8704))
    Fs = []
    for base in (3072, 4096):
        for t in range(4):
            Fs += [base + (4 * t + a) * 64 + r for a in range(4) for r in range(32)]
            Fs += [base + (4 * t + a) * 64 + 32 + r for a in range(4) for r in range(32)]
    for t in range(4):
        Fs += [7168 + (2 * t + a) * 128 + r for a in range(2) for r in range(64)]
        Fs += [7168 + (2 * t + a) * 128 + 64 + r for a in range(2) for r in range(64)]
    Fs += [8192 + a * 128 + r for a in range(2) for r in range(64)]
    Fs += [8192 + a * 128 + 64 + r for a in range(2) for r in range(64)]
    Fs += list(range(2048, 3072)) + list(range(6144, 7168)) + list(range(8704, 9728))
    Fs += list(range(9728, 15872))
    p = np.array(T + Fs, dtype=np.int64)
    assert p.size == IN_COLS and np.unique(p).size == IN_COLS
    return p


class G:
    def __init__(self, nc, stack):
        self.nc = nc
        self.stack = stack
        self.eng = {"pe": nc.tensor, "dve": nc.vector, "act": nc.scalar, "pool": nc.gpsimd, "sp": nc.sync}
        self.esem = {}
        self.ecnt = {}
        self.waited = {}
        self.lastw = {}
        self.readers = {}
        self.dsems = {}
        self.semmax = {}
        self.semobj = {}
        self.dmap = {}
        self.drr = {}
        self.new_epoch()

    def _newsem(self, name):
        s = self.stack.enter_context(self.nc.semaphore(name))
        self.semobj[id(s)] = s
        self.semmax[id(s)] = 0
        return s

    def new_epoch(self):
        n = len(self.semobj)
        for e in self.eng:
            self.esem[e] = self._newsem(f"e_{e}_{n}")
            self.ecnt[e] = 0

    def _wait(self, e, tok):
        sem, val = tok
        k = (e, id(sem))
        if self.waited.get(k, 0) >= val:
            return
        self.eng[e].wait_ge(sem, val)
        self.waited[k] = val

    def op(self, e, fn, R=(), W=(), dma=None):
        toks = []
        for k in R:
            if k in self.lastw:
                toks.append(self.lastw[k])
        for k in W:
            if k in self.lastw:
                toks.append(self.lastw[k])
            toks += list(self.readers.get(k, {}).values())
        own = id(self.esem[e])
        for t in toks:
            if e == "pe" and id(t[0]) == own:
                continue
            self._wait(e, t)
        if dma is not None:
            if dma not in self.dmap:
                npool = 10 if e == "sp" else 8
                idx = self.drr.get(e, 0)
                self.drr[e] = idx + 1
                self.dmap[dma] = "%s%d" % (e, idx % npool)
            dma = self.dmap[dma]
            if dma not in self.dsems:
                self.dsems[dma] = [self._newsem("d%d" % len(self.dsems)), 0]
            ds = self.dsems[dma]
            if ds[1] > 0:
                self._wait(e, (ds[0], ds[1]))
        ins = fn(self.eng[e])
        if dma is not None:
            lst = ins if isinstance(ins, (list, tuple)) else [ins]
            for i in lst:
                i.then_inc(ds[0], 16)
                ds[1] += 16
            tok = (ds[0], ds[1])
        else:
            self.ecnt[e] += 1
            ins.then_inc(self.esem[e], 1)
            tok = (self.esem[e], self.ecnt[e])
        self.semmax[id(tok[0])] = max(self.semmax[id(tok[0])], tok[1])
        for k in R:
            self.readers.setdefault(k, {})[id(tok[0])] = tok
        for k in W:
            self.lastw[k] = tok
            self.readers[k] = {}
        return tok

    def cc(self, fn, R=(), W=()):
        if "cc" not in self.dsems:
            self.dsems["cc"] = [self._newsem("cc"), 0]
        toks = []
        for k in list(R) + list(W):
            if k in self.lastw:
                toks.append(self.lastw[k])
        for k in W:
            toks += list(self.readers.get(k, {}).values())
        for t in toks:
            self._wait("pool", t)
        ds = self.dsems["cc"]
        ins = fn(self.eng["pool"])
        ins.then_inc(ds[0], 1)
        ds[1] += 1
        tok = (ds[0], ds[1])
        self.semmax[id(ds[0])] = ds[1]
        for k in W:
            self.lastw[k] = tok
            self.readers[k] = {}
        for k in R:
            self.readers.setdefault(k, {})[id(tok[0])] = tok
        return tok

    def fence(self):
        for e in self.eng:
            for sid, mx in self.semmax.items():
                if mx > 0:
                    self._wait(e, (self.semobj[sid], mx))
        self.lastw = {}
        self.readers = {}
        self.dmap = {}


def build_program(SEQ, DEPTH):
    Tc = SEQ // NCORE
    NT = Tc // 128
    TQ = Tc + CL
    TT = NT + 1
    QB = min(512, Tc)
    NB = Tc // QB
    NK = NCORE * Tc + CTX
    NKT = NK // 128
    blocks = [(b * QB, QB) for b in range(NB)] + [(Tc, CL)]
    tiles = [(t * 128, 128) for t in range(NT)] + [(Tc, CL)]

    nc = bass.Bass("TRN2", target_bir_lowering=False)
    stack = ExitStack()

    def din(name, shape, dt=F32):
        return nc.dram_tensor(name, list(shape), dt, kind="ExternalInput")

    xT_in = din("xT", [D, Tc])
    cT_in = din("ctxT", [D, CL])
    csc_in = din("csc", [128, KC, 2])
    ng_in = din("ng", [DEPTH, 128, KC])
    bada_in = din("bada", [DEPTH, 128, 48])
    fng_in = din("fng", [128, KC])
    rlr_in = din("rlr", [DEPTH, 16])
    dlam_in = din("dlam", [DEPTH, 256])
    dsg_in = din("dsg", [DEPTH, 128, 1])
    gqg_in = din("gqg", [DEPTH, 128, 4])
    win_in = din("w_in_s", [DEPTH, D // NCORE, IN_COLS])
    wada_in = din("w_ada_s", [DEPTH, D // NCORE, 3 * D])
    wbr_in = din("w_br_s", [DEPTH, 3072 // NCORE, D])
    wout_in = din("w_out_s", [DEPTH, D // NCORE, D])
    cosD_in = din("cosD", [128, TQ]); sinD_in = din("sinD", [128, TQ])
    cosG_in = din("cosG", [128, TQ]); sinG_in = din("sinG", [128, TQ])
    cosR_in = din("cosR", [128, TT, 32]); sinR_in = din("sinR", [128, TT, 32])
    cosK_in = din("cosK", [128, TT, 32]); sinK_in = din("sinK", [128, TT, 32])
    ident_in = din("ident", [128, 128])
    bd_in = din("bdones", [128, 128])
    relf_in = din("relf", [128, 128]); mf_in = din("mf", [128, 128])
    relb_in = din("relb", [128, 128]); mb_in = din("mb", [128, 128])
    ip1_in = din("ip1", [128, 128])
    jx_in = din("jx", [128, 16])
    nexp_in = din("nexp", [128, NT])
    ce_in = din("ce", [128, 9]); cm_in = din("cm", [128, 9])
    crelf_in = din("crelf", [128, 2, 32]); cmf_in = din("cmf", [128, 2, 32])
    crelb_in = din("crelb", [128, 2, 32]); cmb_in = din("cmb", [128, 2, 32])
    jxc_in = din("jxc", [128, 2, 16])
    yT_out = nc.dram_tensor("yT", [D, Tc], F32, kind="ExternalOutput")

    def dscr(name, shape, dt=BF16):
        return nc.dram_tensor(name, list(shape), dt)

    XT = dscr("XT", [KC, 128, TQ], F32)
    WINs = [dscr(f"WINs{l}", [D // NCORE, IN_COLS]) for l in range(DEPTH)]
    WADAs = [dscr(f"WADAs{l}", [D // NCORE, 3 * D]) for l in range(DEPTH)]
    WBRs = [dscr(f"WBRs{l}", [3072 // NCORE, D]) for l in range(DEPTH)]
    WOUTs = [dscr(f"WOUTs{l}", [D // NCORE, D]) for l in range(DEPTH)]
    WIN = [dscr(f"WIN{l}", [D, IN_COLS]) for l in range(DEPTH)]
    WADA = [dscr(f"WADA{l}", [D, 3 * D]) for l in range(DEPTH)]
    WBR = [dscr(f"WBR{l}", [3072, D]) for l in range(DEPTH)]
    WOUT = [dscr(f"WOUT{l}", [D, D]) for l in range(DEPTH)]
    RQ = dscr("RQ", [TQ, 512]); RK = dscr("RK", [TQ, 512]); RV = dscr("RV", [TQ, 1024])
    GSIL = dscr("GSIL", [3072, TQ]); GSIG = dscr("GSIG", [6144, TQ])
    QDT = dscr("QDT", [1024, TQ]); QGT = dscr("QGT", [1024, TQ])
    KTl = [dscr(f"KTl{i}", [1280, TQ]) for i in range(2)]
    KTa = [dscr(f"KTa{i}", [NCORE * 1280, TQ]) for i in range(2)]
    Vl = [dscr(f"Vl{i}", [TQ, 1280]) for i in range(2)]
    Va = [dscr(f"Va{i}", [NCORE * TQ, 1280]) for i in range(2)]
    RCl = [dscr(f"RCl{i}", [CL, 1536]) for i in range(2)]
    RCa = [dscr(f"RCa{i}", [CTX, 1536]) for i in range(2)]
    RAl = [dscr(f"RAl{i}", [1024, 128], F32) for i in range(2)]
    RAa = [dscr(f"RAa{i}", [NCORE * 1024, 128], F32) for i in range(2)]
    BRT = dscr("BRT", [3072, TQ])
    MT = dscr("MT", [D, TQ])

    g = G(nc, stack)
    RG = [list(range(NCORE))]

    uid = [0]

    def sb(st, name, shape, dt):
        uid[0] += 1
        return st.enter_context(nc.sbuf_tensor(f"{name}__{uid[0]}", list(shape), dt))

    PS = [stack.enter_context(nc.psum_tensor(f"ps{i}", [128, 512], F32)) for i in range(8)]
    PSB = [stack.enter_context(nc.psum_tensor(f"psb{i}", [128, 1024], BF16)) for i in range(0)]

    ident = sb(stack, "ident", [128, 128], BF16)
    ones = sb(stack, "ones", [128, 128], BF16)
    bdones = sb(stack, "bdones", [128, 128], BF16)
    epsc = sb(stack, "epsc", [128, 1], F32)
    csc = sb(stack, "csc", [128, KC, 2], F32)
    SC = sb(stack, "SC", [128, KC, 2], BF16)
    fng = sb(stack, "fng", [128, KC], F32)
    def tload(pairs, force_pool=False):
        for (eng, sel) in (("sp", [p for p in pairs if p[0].dtype == p[1].dtype and not force_pool]),
                           ("pool", [p for p in pairs if p[0].dtype != p[1].dtype or force_pool])):
            if sel:
                g.op(eng, lambda e: [e.dma_start(out=tl[:], in_=src) for (tl, src) in sel],
                     W=[tl.name for (tl, _) in sel] + [tl.name.rsplit("__", 1)[0] for (tl, _) in sel], dma="tbl_" + eng)
    tload([(ident, ident_in.ap()), (bdones, bd_in.ap()), (csc, csc_in.ap()), (fng, fng_in.ap())])
    g.op("dve", lambda e: e.memset(ones[:], 1.0), W=["ones"])
    g.op("dve", lambda e: e.memset(epsc[:], EPS), W=["epsc"])
    g.op("act", lambda e: e.activation(out=SC[:], in_=csc[:], func=AF.Silu), R=["csc"], W=["SC"])

    XTv = XT.ap().rearrange("k p t -> p k t")
    g.op("sp", lambda e: e.dma_start(out=XTv[:, :, 0:Tc], in_=xT_in.ap().rearrange("(k p) t -> p k t", p=128)),
         W=["XT"], dma="xinit")
    g.op("sp", lambda e: e.dma_start(out=XTv[:, :, Tc:TQ], in_=cT_in.ap().rearrange("(k p) t -> p k t", p=128)),
         W=["XT"], dma="xinit")

    for l in range(DEPTH):
        for (src, shard, full, rows) in ((win_in, WINs[l], WIN[l], D // NCORE), (wada_in, WADAs[l], WADA[l], D // NCORE),
                                         (wbr_in, WBRs[l], WBR[l], 3072 // NCORE), (wout_in, WOUTs[l], WOUT[l], D // NCORE)):
            step = 32
            for r0 in range(0, rows, step):
                g.op("pool", lambda e, src=src, shard=shard, r0=r0, l=l, step=step:
                     e.dma_start(out=shard.ap()[r0:r0 + step, :], in_=src.ap()[l, r0:r0 + step, :]),
                     W=[shard.name], dma="wcast")
            g.cc(lambda e, shard=shard, full=full: e.collective_compute(
                "AllGather", ALU.bypass, replica_groups=RG, ins=[shard.ap().opt()], outs=[full.ap().opt()]),
                R=[shard.name], W=[full.name])
    g.fence()
    def layer(l):
        par = l % 2
        lam_init = 0.8 - 0.6 * float(np.exp(-0.3 * l))
        with ExitStack() as LS:
            MOD = sb(LS, "MOD", [128, 48, 2], F32)
            gmod = sb(LS, "gmod", [128, KC, 2], F32)
            ngl = sb(LS, "ngl", [128, KC], F32)
            bad = sb(LS, "bad", [128, 48], F32)
            with ExitStack() as S:
                wa = [sb(S, f"wa{i}", [128, KC, 512], BF16) for i in range(2)]
                tload([(ngl, ng_in.ap()[l]), (bad, bada_in.ap()[l])])
                wav = WADA[l].ap().rearrange("(k p) c -> p k c", p=128)
                for gi in range(12):
                    w = wa[gi % 2]
                    g.op("sp", lambda e, w=w, gi=gi: e.dma_start(out=w[:], in_=wav[:, :, gi * 512:(gi + 1) * 512]),
                         W=[w.name], dma=w.name)
                    for j in range(4):
                        jj = gi * 4 + j
                        for k in range(KC):
                            g.op("pe", lambda e, w=w, j=j, jj=jj, k=k: e.matmul(
                                PS[0][:, 2 * jj:2 * jj + 2], w[:, k, j * 128:(j + 1) * 128], SC[:, k, :],
                                start=(k == 0), stop=(k == KC - 1)), R=[w.name, "SC"], W=["ps0"])
                g.op("dve", lambda e: e.tensor_tensor(
                    out=MOD[:], in0=PS[0][:, 0:96].rearrange("p (j c) -> p j c", c=2),
                    in1=bad[:].unsqueeze(2).broadcast_to([128, 48, 2]), op=ALU.add), R=["ps0", "bad"], W=["MOD"])
                g.op("dve", lambda e: e.tensor_scalar(out=gmod[:], in0=MOD[:, 16:32, :], scalar1=1.0, scalar2=None,
                                                      op0=ALU.add), R=["MOD"], W=["gmod"])
                g.op("dve", lambda e: e.tensor_tensor(out=gmod[:], in0=gmod[:],
                                                      in1=ngl[:].unsqueeze(2).broadcast_to([128, KC, 2]), op=ALU.mult),
                     R=["gmod", "ngl"], W=["gmod"])
                g.fence()
                if STOP <= 1:
                    return

            with ExitStack() as S:
                hT = sb(S, "hT", [128, KC, TQ], BF16)
                with ExitStack() as S2:
                    xt = [sb(S2, f"xt{i}", [128, KC, QB], F32) for i in range(2)]
                    sq = [sb(S2, f"sq{i}", [128, KC, QB], BF16) for i in range(2)]
                    sd = sb(S2, "sd", [128, QB], F32)
                    rs = sb(S2, "rs", [128, QB], F32)
                    for bi, (t0, nb) in enumerate(blocks):
                        cx = 1 if t0 >= Tc else 0
                        x_ = xt[bi % 2]; s_ = sq[bi % 2]
                        g.op("sp", lambda e: e.dma_start(out=x_[:, :, :nb], in_=XTv[:, :, t0:t0 + nb]),
                             R=["XT"], W=[x_.name], dma=x_.name)
                        g.op("act", lambda e: e.activation(out=s_[:, :, :nb], in_=x_[:, :, :nb], func=AF.Square),
                             R=[x_.name], W=[s_.name])
                        for k in range(KC):
                            g.op("pe", lambda e, k=k: e.matmul(PS[1][:, :nb], ones[:], s_[:, k, :nb],
                                                               start=(k == 0), stop=(k == KC - 1)),
                                 R=[s_.name, "ones"], W=["ps1"])
                        g.op("act", lambda e: e.activation(out=sd[:, :nb], in_=PS[1][:, :nb], func=AF.Sqrt,
                                                           bias=epsc[:], scale=1.0 / D), R=["ps1", "epsc"], W=["sd"])
                        g.op("dve", lambda e: e.reciprocal(out=rs[:, :nb], in_=sd[:, :nb]), R=["sd"], W=["rs"])
                        g.op("dve", lambda e: e.tensor_tensor(
                            out=x_[:, :, :nb], in0=x_[:, :, :nb],
                            in1=rs[:, :nb].unsqueeze(1).broadcast_to([128, KC, nb]), op=ALU.mult),
                            R=[x_.name, "rs"], W=[x_.name])
                        for k in range(KC):
                            g.op("act", lambda e, k=k: e.activation(
                                out=hT[:, k, t0:t0 + nb], in_=x_[:, k, :nb], func=AF.Identity,
                                bias=MOD[:, k, cx:cx + 1], scale=gmod[:, k, cx:cx + 1]),
                                R=[x_.name, "MOD", "gmod"], W=[("hT", bi, k)])
                    g.fence()
                    if STOP <= 2:
                        return

                WINv = WIN[l].ap().rearrange("(k p) c -> p k c", p=128)
                with ExitStack() as S2:
                    wt = [sb(S2, f"wt{i}", [128, KC, 512], BF16) for i in range(2)]
                    cR = sb(S2, "cR", [128, TT, 32], F32); sR = sb(S2, "sR", [128, TT, 32], F32)
                    cK = sb(S2, "cK", [128, TT, 32], F32); sK = sb(S2, "sK", [128, TT, 32], F32)
                    tmp = [sb(S2, f"tt{i}", [128, 8, 32], F32) for i in range(4)]
                    ob = [sb(S2, f"ob{i}", [128, 512], BF16) for i in range(3)]
                    tload([(cR, cosR_in.ap()), (sR, sinR_in.ap()), (cK, cosK_in.ap()), (sK, sinK_in.ap())])
                    tgroups = [("rq", 0, 512), ("rk", 512, 512), ("rv", 1024, 512), ("rv", 1536, 512),
                               ("dv", 2048, 512), ("dv", 2560, 512), ("gv", 3072, 256)]
                    oi = 0
                    for gi, (kind, c0, ncol) in enumerate(tgroups):
                        w = wt[gi % 2]
                        g.op("sp", lambda e: e.dma_start(out=w[:, :, :ncol], in_=WINv[:, :, c0:c0 + ncol]),
                             W=[w.name], dma=w.name)
                        for ti, (t0, n) in enumerate(tiles):
                            bi = min(t0 // QB, NB); o0 = t0 - blocks[bi][0]
                            pst = PS[2 + (ti % 2)]; pk = f"ps{2 + (ti % 2)}"
                            for k in range(KC):
                                g.op("pe", lambda e, k=k: e.matmul(pst[:n, :ncol], hT[:, k, t0:t0 + n], w[:, k, :ncol],
                                                                   start=(k == 0), stop=(k == KC - 1)),
                                     R=[w.name, ("hT", bi, k)], W=[pk])
                            o_ = ob[oi % 3]; oi += 1
                            if kind in ("rq", "rk"):
                                cc_, ss_ = (cR, sR) if kind == "rq" else (cK, sK)
                                pv = pst[:n, :512].rearrange("p (h c r) -> p h c r", h=8, c=2)
                                ov = o_[:n, :].rearrange("p (h c r) -> p h c r", h=8, c=2)
                                cb = cc_[:n, ti, :].unsqueeze(1).broadcast_to([n, 8, 32])
                                sbb = ss_[:n, ti, :].unsqueeze(1).broadcast_to([n, 8, 32])
                                for i_, (xi, tb) in enumerate(((0, cb), (1, sbb), (0, sbb), (1, cb))):
                                    g.op("dve", lambda e, i_=i_, xi=xi, tb=tb: e.tensor_tensor(
                                        out=tmp[i_][:n], in0=pv[:, :, xi, :], in1=tb, op=ALU.mult),
                                        R=[pk, cc_.name, ss_.name], W=[tmp[i_].name])
                                g.op("pool", lambda e: e.tensor_tensor(out=ov[:, :, 0, :], in0=tmp[0][:n], in1=tmp[1][:n],
                                                                       op=ALU.subtract),
                                     R=[tmp[0].name, tmp[1].name], W=[(o_.name, 0)])
                                g.op("pool", lambda e: e.tensor_tensor(out=ov[:, :, 1, :], in0=tmp[2][:n], in1=tmp[3][:n],
                                                                       op=ALU.add),
                                     R=[tmp[2].name, tmp[3].name], W=[(o_.name, 1)])
                                dst = RQ if kind == "rq" else RK
                                g.op("pool", lambda e: e.dma_start(out=dst.ap()[t0:t0 + n, :], in_=o_[:n, :]),
                                     R=[(o_.name, 0), (o_.name, 1)], dma="st_" + o_.name)
                                if kind == "rk" and t0 >= Tc:
                                    g.op("pool", lambda e: e.dma_start(out=RCl[par].ap()[:, 0:512], in_=o_[:n, :]),
                                         R=[(o_.name, 0), (o_.name, 1)], dma="st_" + o_.name)
                            else:
                                g.op("act", lambda e: e.activation(out=o_[:n, :ncol], in_=pst[:n, :ncol], func=AF.Identity),
                                     R=[pk], W=[(o_.name, 0), (o_.name, 1)])
                                if kind == "rv":
                                    cc0 = c0 - 1024
                                    g.op("pool", lambda e: e.dma_start(out=RV.ap()[t0:t0 + n, cc0:cc0 + ncol], in_=o_[:n, :ncol]),
                                         R=[(o_.name, 0)], dma="st_" + o_.name)
                                    if t0 >= Tc:
                                        g.op("pool", lambda e: e.dma_start(
                                            out=RCl[par].ap()[:, 512 + cc0:512 + cc0 + ncol], in_=o_[:n, :ncol]),
                                            R=[(o_.name, 0)], dma="st_" + o_.name)
                                else:
                                    cc0 = c0 - 2048
                                    g.op("pool", lambda e: e.dma_start(out=Vl[par].ap()[t0:t0 + n, cc0:cc0 + ncol], in_=o_[:n, :ncol]),
                                         R=[(o_.name, 0)], dma="st_" + o_.name)
                    g.fence()
                    if STOP <= 3:
                        return

                with ExitStack() as S2:
                    wf = [sb(S2, f"wf{i}", [128, KC, 512], BF16) for i in range(2)]
                    cDt = sb(S2, "cDt", [128, TQ], F32); sDt = sb(S2, "sDt", [128, TQ], F32)
                    cGt = sb(S2, "cGt", [128, TQ], F32); sGt = sb(S2, "sGt", [128, TQ], F32)
                    gq = sb(S2, "gq", [128, 4], F32)
                    tm = [sb(S2, f"tm{i}", [128, QB], F32) for i in range(4)]
                    an = [sb(S2, f"an{i}", [128, QB], F32) for i in range(2)]
                    sqb = [sb(S2, f"sqb{i}", [128, QB], BF16) for i in range(2)]
                    sdg = sb(S2, "sdg", [128, QB], F32); rsg = sb(S2, "rsg", [128, QB], F32)
                    o1 = [sb(S2, f"o1_{i}", [128, QB], BF16) for i in range(2)]
                    o2 = [sb(S2, f"o2_{i}", [128, QB], BF16) for i in range(2)]
                    go = [sb(S2, f"go{i}", [128, 4, QB], BF16) for i in range(2)]
                    tload([(cDt, cosD_in.ap()), (sDt, sinD_in.ap()), (cGt, cosG_in.ap()), (sGt, sinG_in.ap()), (gq, gqg_in.ap()[l])])
                    FB = 3328
                    fgroups = [("dq", 0, 4), ("dq", 4, 4), ("dk", 8, 4), ("dk", 12, 4), ("gq", 16, 4), ("gq", 20, 4),
                               ("gk", 24, 2)]
                    fgroups += [("sil", 26 + 4 * i, 4) for i in range(6)] + [("sig", 50 + 4 * i, 4) for i in range(12)]
                    pi = 0
                    KF = os.environ.get('KF')
                    if KF:
                        fgroups = [fgroups[int(i)] for i in KF.split(',')]
                    for gi, (kind, tile0, ntl) in enumerate(fgroups):
                        w = wf[gi % 2]
                        ncol = ntl * 128
                        g.op("sp", lambda e: e.dma_start(out=w[:, :, :ncol], in_=WINv[:, :, FB + tile0 * 128:FB + tile0 * 128 + ncol]),
                             W=[w.name], dma=w.name)
                        for bi, (t0, nb) in enumerate(blocks):
                            def mm(ps_, pk_, j):
                                for k in range(KC):
                                    g.op("pe", lambda e, k=k: e.matmul(ps_[:, :nb], w[:, k, j * 128:(j + 1) * 128],
                                                                       hT[:, k, t0:t0 + nb], start=(k == 0), stop=(k == KC - 1)),
                                         R=[w.name, ("hT", bi, k)], W=[pk_])
                            if kind in ("sil", "sig"):
                                go_ = go[pi % 2]; pi += 1
                                for j in range(ntl):
                                    bk = 4 + (j % 2)
                                    mm(PS[bk], f"ps{bk}", j)
                                    g.op("act", lambda e, j=j, bk=bk: e.activation(
                                        out=go_[:, j, :nb], in_=PS[bk][:, :nb], func=(AF.Identity if os.environ.get("KACT") else (AF.Silu if kind == "sil" else AF.Sigmoid))),
                                        R=[f"ps{bk}"], W=[(go_.name, j)])
                                if kind == "sil":
                                    r0 = (tile0 - 26) * 128; dst = GSIL
                                else:
                                    r0 = (tile0 - 50) * 128; dst = GSIG
                                g.op("pool", lambda e: [e.dma_start(
                                    out=dst.ap()[r0 + j * 128:r0 + (j + 1) * 128, t0:t0 + nb],
                                    in_=go_[:, j, :nb]) for j in range(ntl)], R=[(go_.name, j) for j in range(ntl)],
                                    dma="st_" + go_.name)
                                continue
                            isd = kind in ("dq", "dk")
                            ct, st_ = (cDt, sDt) if isd else (cGt, sGt)
                            for pr in range(ntl // 2):
                                pa, pb = (0, 1) if pi % 2 == 0 else (2, 3)
                                o1_ = o1[pi % 2]; o2_ = o2[pi % 2]; pi += 1
                                mm(PS[pa], f"ps{pa}", 2 * pr)
                                mm(PS[pb], f"ps{pb}", 2 * pr + 1)
                                if isd:
                                    A = PS[pa][:, :nb]; B = PS[pb][:, :nb]; ra = [f"ps{pa}"]; rb = [f"ps{pb}"]
                                else:
                                    gc = 0 if kind == "gq" else 2
                                    g.op("act", lambda e: e.activation(out=sqb[0][:, :nb], in_=PS[pa][:, :nb], func=AF.Square),
                                         R=[f"ps{pa}"], W=["sqb0"])
                                    g.op("act", lambda e: e.activation(out=sqb[1][:, :nb], in_=PS[pb][:, :nb], func=AF.Square),
                                         R=[f"ps{pb}"], W=["sqb1"])
                                    g.op("pe", lambda e: e.matmul(PS[6][:, :nb], bdones[:], sqb[0][:, :nb], start=True, stop=False),
                                         R=["sqb0", "bdones"], W=["ps6"])
                                    g.op("pe", lambda e: e.matmul(PS[6][:, :nb], bdones[:], sqb[1][:, :nb], start=False, stop=True),
                                         R=["sqb1", "bdones"], W=["ps6"])
                                    g.op("act", lambda e: e.activation(out=sdg[:, :nb], in_=PS[6][:, :nb], func=AF.Sqrt,
                                                                       bias=epsc[:], scale=1.0 / 128), R=["ps6", "epsc"], W=["sdg"])
                                    g.op("dve", lambda e: e.reciprocal(out=rsg[:, :nb], in_=sdg[:, :nb]), R=["sdg"], W=["rsg"])
                                    g.op("dve", lambda e: e.scalar_tensor_tensor(
                                        out=an[0][:, :nb], in0=PS[pa][:, :nb], scalar=gq[:, gc:gc + 1], in1=rsg[:, :nb],
                                        op0=ALU.mult, op1=ALU.mult), R=[f"ps{pa}", "gq", "rsg"], W=["an0"])
                                    g.op("dve", lambda e: e.scalar_tensor_tensor(
                                        out=an[1][:, :nb], in0=PS[pb][:, :nb], scalar=gq[:, gc + 1:gc + 2], in1=rsg[:, :nb],
                                        op0=ALU.mult, op1=ALU.mult), R=[f"ps{pb}", "gq", "rsg"], W=["an1"])
                                    A = an[0][:, :nb]; B = an[1][:, :nb]; ra = ["an0"]; rb = ["an1"]
                                cs = ct[:, t0:t0 + nb]; sn = st_[:, t0:t0 + nb]
                                for i_, (X, rx, tb) in enumerate(((A, ra, cs), (B, rb, sn), (A, ra, sn), (B, rb, cs))):
                                    g.op("dve", lambda e, i_=i_, X=X, tb=tb: e.tensor_tensor(
                                        out=tm[i_][:, :nb], in0=X, in1=tb, op=ALU.mult),
                                        R=rx + [ct.name, st_.name], W=[tm[i_].name])
                                g.op("pool", lambda e: e.tensor_tensor(out=o1_[:, :nb], in0=tm[0][:, :nb], in1=tm[1][:, :nb],
                                                                       op=ALU.subtract), R=[tm[0].name, tm[1].name], W=[o1_.name])
                                g.op("pool", lambda e: e.tensor_tensor(out=o2_[:, :nb], in0=tm[2][:, :nb], in1=tm[3][:, :nb],
                                                                       op=ALU.add), R=[tm[2].name, tm[3].name], W=[o2_.name])
                                if isd:
                                    t_ = (tile0 % 8) // 2 + pr
                                    dstT = QDT if kind == "dq" else KTl[par]
                                    def st(e):
                                        ins = []
                                        for a in range(4):
                                            r0 = (4 * t_ + a) * 64
                                            ins.append(e.dma_start(out=dstT.ap()[r0:r0 + 32, t0:t0 + nb], in_=o1_[32 * a:32 * a + 32, :nb]))
                                            ins.append(e.dma_start(out=dstT.ap()[r0 + 32:r0 + 64, t0:t0 + nb], in_=o2_[32 * a:32 * a + 32, :nb]))
                                        return ins
                                else:
                                    t_ = ((tile0 - 16) // 2 + pr) if kind == "gq" else 0
                                    dstT = QGT if kind == "gq" else KTl[par]
                                    rbase = 0 if kind == "gq" else 1024
                                    def st(e):
                                        ins = []
                                        for a in range(2):
                                            r0 = rbase + (2 * t_ + a) * 128
                                            ins.append(e.dma_start(out=dstT.ap()[r0:r0 + 64, t0:t0 + nb], in_=o1_[64 * a:64 * a + 64, :nb]))
                                            ins.append(e.dma_start(out=dstT.ap()[r0 + 64:r0 + 128, t0:t0 + nb], in_=o2_[64 * a:64 * a + 64, :nb]))
                                        return ins
                                g.op("pool", st, R=[o1_.name, o2_.name], dma="st_" + o1_.name)
                    g.fence()
                    if STOP <= 4:
                        return

            with ExitStack() as S:
                Sloc = sb(S, "Sloc", [128, NT, 8, 128], F32)
                rl = sb(S, "rl", [128, 16], F32)
                LG16 = sb(S, "LG16", [128, 16], F32)
                LGFB = sb(S, "LGFB", [128, 8], F32)
                jx = sb(S, "jx", [128, 16], F32)
                W16 = sb(S, "W16", [128, 16], F32)
                GS = sb(S, "GS", [128, 8], F32)
                GSb = sb(S, "GSb", [128, 8, 128], F32)
                tload([(rl, rlr_in.ap()[l:l + 1, :].partition_broadcast(128))], force_pool=True)
                tload([(jx, jx_in.ap())])
                g.op("act", lambda e: e.activation(out=LG16[:], in_=rl[:], func=AF.Exp), R=["rl"], W=["LG16"])
                g.op("dve", lambda e: e.tensor_scalar(out=LG16[:], in0=LG16[:], scalar1=-1.0, scalar2=None, op0=ALU.mult),
                     R=["LG16"], W=["LG16"])
                g.op("dve", lambda e: e.tensor_copy(out=LGFB[0:64, :], in_=LG16[0:64, 0:8]), R=["LG16"], W=["LGFBa"])
                g.op("dve", lambda e: e.tensor_copy(out=LGFB[64:128, :], in_=LG16[64:128, 8:16]), R=["LG16"], W=["LGFBb"])
                g.op("dve", lambda e: e.tensor_tensor(out=W16[:], in0=LG16[:], in1=jx[:], op=ALU.mult), R=["LG16", "jx"], W=["W16"])
                g.op("act", lambda e: e.activation(out=W16[:], in_=W16[:], func=AF.Exp), R=["W16"], W=["W16"])
                W16b = sb(S, "W16b", [128, 16], BF16)
                g.op("dve", lambda e: e.tensor_copy(out=W16b[:], in_=W16[:]), R=["W16"], W=["W16b"])
                g.op("act", lambda e: e.activation(out=GS[:], in_=LGFB[:], func=AF.Exp, scale=128.0), R=["LGFBa", "LGFBb"], W=["GS"])
                g.op("dve", lambda e: e.tensor_copy(out=GSb[:], in_=GS[:].unsqueeze(2).broadcast_to([128, 8, 128])), R=["GS"], W=["GSb"])
                if os.environ.get("KR1") == "a":
                    g.fence()
                    return
                with ExitStack() as S2:
                    kt_ = [sb(S2, f"rk{i}", [128, 512], BF16) for i in range(2)]
                    vt_ = [sb(S2, f"rv{i}", [128, 1024], BF16) for i in range(2)]
                    kw = [sb(S2, f"kw{i}", [128, 8, 128], BF16) for i in range(2)]
                    car = [sb(S2, f"car{i}", [128, 8, 128], F32) for i in range(2)]
                    for n in range(NT):
                        k_ = kt_[n % 2]; v_ = vt_[n % 2]; w_ = kw[n % 2]
                        g.op("sp", lambda e: e.dma_start(out=k_[:], in_=RK.ap()[n * 128:(n + 1) * 128, :]), W=[k_.name], dma=k_.name)
                        g.op("sp", lambda e: e.dma_start(out=v_[:], in_=RV.ap()[n * 128:(n + 1) * 128, :]), W=[v_.name], dma=v_.name)
                        kv = k_[:].rearrange("p (h d) -> p h d", h=8)
                        for dr in range(2):
                            g.op("dve", lambda e, dr=dr: e.tensor_tensor(
                                out=w_[:, :, dr * 64:(dr + 1) * 64], in0=kv, in1=W16b[:, dr * 8:(dr + 1) * 8].unsqueeze(2).broadcast_to([128, 8, 64]),
                                op=ALU.mult), R=[k_.name, "W16b"], W=[(w_.name, dr)])
                        if os.environ.get("KB") == "1":
                            continue
                        for h in range(8):
                            bk = 2 + h // 4
                            g.op("pe", lambda e, h=h, bk=bk: e.matmul(
                                PS[bk][:, (h % 4) * 128:(h % 4 + 1) * 128], w_[:, h, :],
                                v_[:, h * 128:(h + 1) * 128], start=True, stop=True),
                                R=[(w_.name, 0), (w_.name, 1), v_.name], W=[f"ps{bk}"])
                        if os.environ.get("KB") == "2":
                            continue
                        for hb in range(2):
                            g.op("act", lambda e, hb=hb: e.activation(
                                out=Sloc[:, n, hb * 4:(hb + 1) * 4, :].rearrange("p h e -> p (h e)"), in_=PS[2 + hb][:, :],
                                func=AF.Identity), R=[f"ps{2 + hb}"], W=[("Sloc", n, hb)])
                    if os.environ.get("KR1") == "b":
                        g.fence()
                        return
                    for (lo, order) in ((0, list(range(NT))), (64, list(range(NT - 1, -1, -1)))):
                        hi = lo + 64
                        g.op("dve", lambda e: e.memset(car[0][lo:hi], 0.0), W=[(car[0].name, lo)])
                        ci = 0
                        for n in order:
                            c_ = car[ci % 2]; cn = car[(ci + 1) % 2]; ci += 1
                            g.op("dve", lambda e: e.tensor_tensor(out=cn[lo:hi], in0=c_[lo:hi], in1=GSb[lo:hi], op=ALU.mult),
                                 R=[(c_.name, lo), "GSb"], W=[(cn.name, lo)])
                            g.op("dve", lambda e: e.tensor_tensor(out=cn[lo:hi], in0=cn[lo:hi], in1=Sloc[lo:hi, n], op=ALU.add),
                                 R=[(cn.name, lo), ("Sloc", n, 0), ("Sloc", n, 1)], W=[(cn.name, lo)])
                            g.op("pool", lambda e: e.tensor_copy(out=Sloc[lo:hi, n], in_=c_[lo:hi]),
                                 R=[(c_.name, lo), (cn.name, lo)], W=[("Sloc", n, 0), ("Sloc", n, 1), ("SlocF", n, lo)])
                        cf = car[ci % 2]
                        g.op("pool", lambda e: [e.dma_start(out=RAl[par].ap()[h * 128 + lo:h * 128 + hi, :], in_=cf[lo:hi, h, :])
                                                for h in range(8)],
                            R=[(cf.name, lo)], dma=f"st_ra{lo}")
                g.fence()
                if os.environ.get("KR1"):
                    return
                g.cc(lambda e: e.collective_compute("AllGather", ALU.bypass, replica_groups=RG,
                                                    ins=[RAl[par].ap().opt()], outs=[RAa[par].ap().opt()]), R=["RAl"], W=["RAa"])
                g.cc(lambda e: e.collective_compute("AllGather", ALU.bypass, replica_groups=RG,
                                                    ins=[RCl[par].ap().opt()], outs=[RCa[par].ap().opt()]), R=["RCl"], W=["RCa"])
                g.cc(lambda e: e.collective_compute("AllGather", ALU.bypass, replica_groups=RG,
                                                    ins=[KTl[par].ap().opt()], outs=[KTa[par].ap().opt()]), R=["KTl"], W=["KTa"])
                g.cc(lambda e: e.collective_compute("AllGather", ALU.bypass, replica_groups=RG,
                                                    ins=[Vl[par].ap().opt()], outs=[Va[par].ap().opt()]), R=["Vl"], W=["Va"])
                g.fence()
                if STOP <= 5:
                    return

                with ExitStack() as S2:
                    DT = sb(S2, "DT", [128, 8, 128], F32)
                    CFB = sb(S2, "CFB", [128, 8, 128], F32)
                    GP = sb(S2, "GP", [128, NT, 8], F32)
                    COEF = sb(S2, "COEF", [128, 8, 9], F32)
                    DCT = sb(S2, "DCT", [128, 2, 8, 32], F32)
                    WC = sb(S2, "WC", [128, 2, 16], F32)
                    relf = sb(S2, "relf", [128, 128], F32); mf = sb(S2, "mf", [128, 128], F32)
                    relb = sb(S2, "relb", [128, 128], F32); mb = sb(S2, "mb", [128, 128], F32)
                    ip1 = sb(S2, "ip1", [128, 128], F32)
                    nexp = sb(S2, "nexp", [128, NT], F32)
                    ce = sb(S2, "ce", [128, 9], F32); cm = sb(S2, "cm", [128, 9], F32)
                    crelf = sb(S2, "crelf", [128, 2, 32], F32); cmf = sb(S2, "cmf", [128, 2, 32], F32)
                    crelb = sb(S2, "crelb", [128, 2, 32], F32); cmb = sb(S2, "cmb", [128, 2, 32], F32)
                    jxc = sb(S2, "jxc", [128, 2, 16], F32)
                    e1 = sb(S2, "e1", [128, 128], F32); e2 = sb(S2, "e2", [128, 128], F32)
                    tload([(tl, src.ap()) for (tl, src) in ((relf, relf_in), (mf, mf_in), (relb, relb_in), (mb, mb_in), (ip1, ip1_in),
                                                            (nexp, nexp_in), (ce, ce_in), (cm, cm_in), (crelf, crelf_in), (cmf, cmf_in),
                                                            (crelb, crelb_in), (cmb, cmb_in), (jxc, jxc_in))])
                    LGk = ["LGFBa", "LGFBb"]
                    for h in range(8):
                        g.op("act", lambda e, h=h: e.activation(out=e1[:], in_=relf[:], func=AF.Exp, scale=LG16[:, h:h + 1]),
                             R=["relf", "LG16"], W=["e1"])
                        g.op("dve", lambda e: e.tensor_tensor(out=e1[:], in0=e1[:], in1=mf[:], op=ALU.mult), R=["e1", "mf"], W=["e1"])
                        g.op("act", lambda e, h=h: e.activation(out=e2[:], in_=relb[:], func=AF.Exp, scale=LG16[:, 8 + h:9 + h]),
                             R=["relb", "LG16"], W=["e2"])
                        g.op("dve", lambda e: e.tensor_tensor(out=e2[:], in0=e2[:], in1=mb[:], op=ALU.mult), R=["e2", "mb"], W=["e2"])
                        g.op("dve", lambda e, h=h: e.tensor_tensor(out=DT[:, h, :], in0=e1[:], in1=e2[:], op=ALU.add),
                             R=["e1", "e2"], W=["DT"])
                        g.op("act", lambda e, h=h: e.activation(out=CFB[:, h, :], in_=ip1[:], func=AF.Exp, scale=LGFB[:, h:h + 1]),
                             R=["ip1"] + LGk, W=["CFB"])
                        g.op("act", lambda e, h=h: e.activation(out=GP[:, :, h], in_=nexp[:], func=AF.Exp, scale=LGFB[:, h:h + 1]),
                             R=["nexp"] + LGk, W=["GP"])
                        g.op("act", lambda e, h=h: e.activation(out=COEF[:, h, :], in_=ce[:], func=AF.Exp, scale=LGFB[:, h:h + 1]),
                             R=["ce"] + LGk, W=["COEF"])
                        for jt in range(2):
                            g.op("act", lambda e, h=h, jt=jt: e.activation(out=e1[:, 0:32], in_=crelf[:, jt, :], func=AF.Exp,
                                                                          scale=LG16[:, h:h + 1]), R=["crelf", "LG16"], W=["e1"])
                            g.op("dve", lambda e, jt=jt: e.tensor_tensor(out=e1[:, 0:32], in0=e1[:, 0:32], in1=cmf[:, jt, :], op=ALU.mult),
                                 R=["e1", "cmf"], W=["e1"])
                            g.op("act", lambda e, h=h, jt=jt: e.activation(out=e2[:, 0:32], in_=crelb[:, jt, :], func=AF.Exp,
                                                                          scale=LG16[:, 8 + h:9 + h]), R=["crelb", "LG16"], W=["e2"])
                            g.op("dve", lambda e, jt=jt: e.tensor_tensor(out=e2[:, 0:32], in0=e2[:, 0:32], in1=cmb[:, jt, :], op=ALU.mult),
                                 R=["e2", "cmb"], W=["e2"])
                            g.op("dve", lambda e, h=h, jt=jt: e.tensor_tensor(out=DCT[:, jt, h, :], in0=e1[:, 0:32], in1=e2[:, 0:32], op=ALU.add),
                                 R=["e1", "e2"], W=["DCT"])
                    g.op("dve", lambda e: e.tensor_tensor(out=COEF[:], in0=COEF[:], in1=cm[:].unsqueeze(1).broadcast_to([128, 8, 9]),
                                                          op=ALU.mult), R=["COEF", "cm"], W=["COEF"])
                    g.op("dve", lambda e: e.tensor_tensor(out=WC[:], in0=jxc[:], in1=LG16[:].unsqueeze(1).broadcast_to([128, 2, 16]),
                                                          op=ALU.mult), R=["jxc", "LG16"], W=["WC"])
                    g.op("act", lambda e: e.activation(out=WC[:], in_=WC[:], func=AF.Exp), R=["WC"], W=["WC"])

                    kC = sb(S2, "kC", [128, 2, 512], BF16); vC = sb(S2, "vC", [128, 2, 1024], BF16)
                    kwc = sb(S2, "kwc", [128, 2, 8, 128], BF16)
                    SCTX = sb(S2, "SCTX", [128, 8, 128], F32)
                    Sin = sb(S2, "Sin", [128, 8, 128], F32)
                    SBF = sb(S2, "SBF", [128, NT, 8, 128], BF16)
                    S3 = ExitStack()
                    RA = sb(S3, "RA", [128, NCORE, 8, 128], F32)
                    RCv = RCa[par].ap().rearrange("(j p) c -> p j c", p=128)
                    tload([(kC, RCv[:, :, 0:512]), (vC, RCv[:, :, 512:1536])])
                    g.op("sp", lambda e: [e.dma_start(out=RA[:, c_], in_=RAa[par].ap()[c_ * 1024:(c_ + 1) * 1024, :].rearrange(
                        "(h p) e -> p h e", p=128)) for c_ in range(NCORE)], W=["RA"], dma="tbl_sp")
                    for jt in range(2):
                        for dr in range(2):
                            g.op("dve", lambda e, jt=jt, dr=dr: e.tensor_tensor(
                                out=kwc[:, jt, :, dr * 64:(dr + 1) * 64], in0=kC[:, jt, :].rearrange("p (h d) -> p h d", h=8),
                                in1=WC[:, jt, dr * 8:(dr + 1) * 8].unsqueeze(2).broadcast_to([128, 8, 64]), op=ALU.mult),
                                R=["kC", "WC"], W=[("kwc", jt, dr)])
                    for h in range(8):
                        bk = 2 + h // 4
                        for jt in range(2):
                            g.op("pe", lambda e, h=h, jt=jt, bk=bk: e.matmul(
                                PS[bk][:, (h % 4) * 128:(h % 4 + 1) * 128], kwc[:, jt, h, :],
                                vC[:, jt, h * 128:(h + 1) * 128], start=(jt == 0), stop=(jt == 1)),
                                R=[("kwc", jt, 0), ("kwc", jt, 1), "vC"], W=[f"ps{bk}"])
                    for hb in range(2):
                        g.op("act", lambda e, hb=hb: e.activation(out=SCTX[:, hb * 4:(hb + 1) * 4, :].rearrange("p h e -> p (h e)"),
                                                                  in_=PS[2 + hb][:, :], func=AF.Identity), R=[f"ps{2 + hb}"], W=[("SCTX", hb)])
                    for h in range(8):
                        g.op("dve", lambda e, h=h: e.tensor_scalar(out=Sin[:, h, :], in0=SCTX[:, h, :], scalar1=COEF[:, h, 8:9],
                                                                   scalar2=None, op0=ALU.mult),
                             R=[("SCTX", h // 4), "COEF"], W=[("Sin", h)])
                        for c_ in range(NCORE):
                            g.op("dve", lambda e, h=h, c_=c_: e.scalar_tensor_tensor(
                                out=Sin[:, h, :], in0=RA[:, c_, h, :], scalar=COEF[:, h, c_:c_ + 1], in1=Sin[:, h, :],
                                op0=ALU.mult, op1=ALU.add), R=["RA", "COEF", ("Sin", h)], W=[("Sin", h)])
                    g.fence()
                    S3.close()
                    for n in range(NT):
                        for h in range(8):
                            eng = "dve"
                            g.op(eng, lambda e, n=n, h=h: e.scalar_tensor_tensor(
                                out=SBF[:, n, h, :], in0=Sin[:, h, :], scalar=GP[:, n, h:h + 1], in1=Sloc[:, n, h, :],
                                op0=ALU.mult, op1=ALU.add),
                                R=[("Sin", h), "GP", ("SlocF", n, 0), ("SlocF", n, 64), ("Sloc", n, 0), ("Sloc", n, 1)], W=[("SBF", n)])

                    if os.environ.get("KR2") == "a":
                        g.fence()
                        return
                    qt_ = [sb(S2, f"q2{i}", [128, 512], BF16) for i in range(2)]
                    kt2 = [sb(S2, f"k2{i}", [128, 512], BF16) for i in range(2)]
                    vt2 = [sb(S2, f"v2{i}", [128, 1024], BF16) for i in range(2)]
                    gt2 = [sb(S2, f"g2{i}", [128, 8, 128], BF16) for i in range(2)]
                    qdup = sb(S2, "qdup", [128, 8, 128], BF16)
                    QC = sb(S2, "QC", [128, 8, 128], BF16)
                    QT2 = sb(S2, "QT2", [128, 8, 128], BF16)
                    KT2 = sb(S2, "KT2", [128, 4, 128], BF16)
                    HM = sb(S2, "HM", [128, 8, 128], BF16)
                    QT2m = sb(S2, "QT2m", [128, 8, 128], BF16)
                    QTcm = sb(S2, "QTcm", [128, 8, CL], BF16)
                    HMv = HM[:].rearrange("p (a b) t -> p a b t", b=2)
                    for (lo_, b_, val) in ((0, 0, 1.0), (0, 1, 0.0), (64, 0, 0.0), (64, 1, 1.0)):
                        g.op("dve", lambda e, lo_=lo_, b_=b_, val=val: e.memset(HMv[lo_:lo_ + 64, :, b_, :], val), W=[("HM", lo_, b_)])
                    HMk = [("HM", 0, 0), ("HM", 0, 1), ("HM", 64, 0), ("HM", 64, 1)]
                    attm = sb(S2, "attm", [128, 8, 128], BF16)
                    osb = sb(S2, "osb", [128, 8, 128], F32)
                    osq = sb(S2, "osq", [128, 8, 128], F32)
                    ss8 = sb(S2, "ss8", [128, 8], F32); rs8 = sb(S2, "rs8", [128, 8], F32)
                    onb = sb(S2, "onb", [128, 8, 128], BF16)
                    brs = [sb(S2, f"brs{i}", [128, 8, 128], BF16) for i in range(2)]
                    GSILv = GSIL.ap()[0:1024, :].rearrange("(h p) t -> p h t", p=128)
                    BRTv = BRT.ap()[0:1024, :].rearrange("(h p) t -> p h t", p=128)

                    def finish(n_, t0, i2):
                        for hb in range(2):
                            g.op("act", lambda e, hb=hb: e.activation(
                                out=osb[:n_, hb * 4:(hb + 1) * 4, :].rearrange("p h e -> p (h e)"), in_=PS[4 + hb][:n_, :],
                                func=AF.Identity), R=[f"ps{4 + hb}"], W=[("osb", hb)])
                        g.op("dve", lambda e: e.tensor_tensor(out=osq[:n_], in0=osb[:n_], in1=osb[:n_], op=ALU.mult),
                             R=[("osb", 0), ("osb", 1)], W=["osq"])
                        g.op("dve", lambda e: e.tensor_reduce(out=ss8[:n_], in_=osq[:n_], axis=AX.X, op=ALU.add), R=["osq"], W=["ss8"])
                        g.op("act", lambda e: e.activation(out=ss8[:n_], in_=ss8[:n_], func=AF.Sqrt, bias=epsc[:n_], scale=1.0 / 128),
                             R=["ss8", "epsc"], W=["ss8"])
                        g.op("dve", lambda e: e.reciprocal(out=rs8[:n_], in_=ss8[:n_]), R=["ss8"], W=["rs8"])
                        g.op("dve", lambda e: e.tensor_tensor(out=onb[:n_], in0=osb[:n_],
                                                              in1=rs8[:n_].unsqueeze(2).broadcast_to([n_, 8, 128]), op=ALU.mult),
                             R=[("osb", 0), ("osb", 1), "rs8"], W=["onb"])
                        for h in range(8):
                            g.op("pe", lambda e, h=h: e.matmul(PS[(6, 1)[h // 4]][:, (h % 4) * 128:(h % 4) * 128 + n_], onb[:n_, h, :],
                                                               ident[:n_, :n_], start=True, stop=True),
                                 R=["onb", "ident"], W=[f"ps{(6, 1)[h // 4]}"])
                        g_ = gt2[i2 % 2]; b_ = brs[i2 % 2]
                        g.op("sp", lambda e: e.dma_start(out=g_[:, :, :n_], in_=GSILv[:, :, t0:t0 + n_]), W=[g_.name], dma=g_.name)
                        for hb in range(2):
                            g.op("dve", lambda e, hb=hb: e.tensor_tensor(
                                out=b_[:, hb * 4:(hb + 1) * 4, :n_], in0=PS[(6, 1)[hb]][:, :].rearrange("p (h t) -> p h t", h=4)[:, :, :n_],
                                in1=g_[:, hb * 4:(hb + 1) * 4, :n_], op=ALU.mult), R=[f"ps{(6, 1)[hb]}", g_.name], W=[(b_.name, hb)])
                        g.op("pool", lambda e: [e.dma_start(out=BRT.ap()[h * 128:(h + 1) * 128, t0:t0 + n_], in_=b_[:, h, :n_])
                                                for h in range(8)], R=[(b_.name, 0), (b_.name, 1)],
                             dma="st_" + b_.name)

                    for n in range(NT):
                        q_ = qt_[n % 2]; k_ = kt2[n % 2]; v_ = vt2[n % 2]
                        g.op("sp", lambda e: e.dma_start(out=q_[:], in_=RQ.ap()[n * 128:(n + 1) * 128, :]), W=[q_.name], dma=q_.name)
                        g.op("sp", lambda e: e.dma_start(out=k_[:], in_=RK.ap()[n * 128:(n + 1) * 128, :]), W=[k_.name], dma=k_.name)
                        g.op("sp", lambda e: e.dma_start(out=v_[:], in_=RV.ap()[n * 128:(n + 1) * 128, :]), W=[v_.name], dma=v_.name)
                        for dr in range(2):
                            g.op("dve", lambda e, dr=dr: e.tensor_copy(
                                out=qdup[:, :, dr * 64:(dr + 1) * 64], in_=q_[:].rearrange("p (h d) -> p h d", h=8)),
                                R=[q_.name], W=[("qdup", dr)])
                        for h in range(8):
                            g.op("pe", lambda e, h=h: e.matmul(PS[(6, 1)[h // 4]][:, (h % 4) * 128:(h % 4 + 1) * 128],
                                                               qdup[:, h, :], ident[:], start=True, stop=True),
                                 R=[("qdup", 0), ("qdup", 1), "ident"], W=[f"ps{(6, 1)[h // 4]}"])
                        for hb in range(2):
                            g.op("act", lambda e, hb=hb: e.activation(
                                out=QT2[:, hb * 4:(hb + 1) * 4, :].rearrange("p h t -> p (h t)"), in_=PS[(6, 1)[hb]][:, :], func=AF.Identity),
                                R=[f"ps{(6, 1)[hb]}"], W=[("QT2", hb)])
                            g.op("dve", lambda e, hb=hb: e.tensor_tensor(
                                out=QC[:, hb * 4:(hb + 1) * 4, :], in0=QT2[:, hb * 4:(hb + 1) * 4, :],
                                in1=CFB[:, hb * 4:(hb + 1) * 4, :], op=ALU.mult), R=[("QT2", hb), "CFB"], W=[("QC", hb)])
                            g.op("dve", lambda e, hb=hb: e.tensor_tensor(
                                out=QT2m[:, hb * 4:(hb + 1) * 4, :], in0=QT2[:, hb * 4:(hb + 1) * 4, :],
                                in1=HM[:, hb * 4:(hb + 1) * 4, :], op=ALU.mult), R=[("QT2", hb)] + HMk, W=[("QT2m", hb)])
                        for j in range(4):
                            g.op("pe", lambda e, j=j: e.matmul(PS[0][:, j * 128:(j + 1) * 128], k_[:, j * 128:(j + 1) * 128], ident[:],
                                                               start=True, stop=True),
                                 R=[k_.name, "ident"], W=["ps0"])
                        g.op("act", lambda e: e.activation(out=KT2[:].rearrange("p j t -> p (j t)"), in_=PS[0][:, :], func=AF.Identity),
                             R=["ps0"], W=["KT2"])
                        if os.environ.get("KC2") == "1":
                            continue
                        for h in range(8):
                            off = 64 * (h % 2); bk = 2 + h // 4
                            g.op("pe", lambda e, h=h, off=off, bk=bk: e.matmul(
                                PS[bk][:, (h % 4) * 128:(h % 4 + 1) * 128], KT2[:, h // 2, :], QT2m[:, h, :],
                                start=True, stop=True), R=["KT2", ("QT2m", 0), ("QT2m", 1)], W=[f"ps{bk}"])
                        for hb in range(2):
                            g.op("dve", lambda e, hb=hb: e.tensor_tensor(
                                out=attm[:, hb * 4:(hb + 1) * 4, :], in0=PS[2 + hb][:, :].rearrange("p (h t) -> p h t", h=4),
                                in1=DT[:, hb * 4:(hb + 1) * 4, :], op=ALU.mult), R=[f"ps{2 + hb}", "DT"], W=[("attm", hb)])
                        if os.environ.get("KC2") == "2":
                            continue
                        for h in range(8):
                            bk = 4 + h // 4
                            sl = slice((h % 4) * 128, (h % 4 + 1) * 128)
                            g.op("pe", lambda e, h=h, bk=bk, sl=sl: e.matmul(PS[bk][:, sl], attm[:, h, :], v_[:, h * 128:(h + 1) * 128],
                                                                             start=True, stop=False),
                                 R=[("attm", h // 4), v_.name], W=[f"ps{bk}"])
                            g.op("pe", lambda e, h=h, bk=bk, sl=sl: e.matmul(PS[bk][:, sl], QC[:, h, :], SBF[:, n, h, :],
                                                                             start=False, stop=True),
                                 R=[("QC", 0), ("QC", 1), ("SBF", n)], W=[f"ps{bk}"])
                        if os.environ.get("KC2") == "3":
                            continue
                        finish(128, n * 128, n)

                    if os.environ.get("KR2") == "b":
                        g.fence()
                        return
                    qc_ = sb(S2, "qc_", [CL, 512], BF16)
                    QTc = sb(S2, "QTc", [128, 4, CL], BF16)
                    KTc = sb(S2, "KTc", [128, 4, 2, 128], BF16)
                    attc = sb(S2, "attc", [128, 2, 8, CL], BF16)
                    tload([(qc_, RQ.ap()[Tc:TQ, :])])
                    for j in range(4):
                        g.op("pe", lambda e, j=j: e.matmul(PS[0][:, j * CL:(j + 1) * CL], qc_[:, j * 128:(j + 1) * 128], ident[:CL, :CL],
                                                           start=True, stop=True),
                             R=["qc_", "ident"], W=["ps0"])
                    g.op("act", lambda e: e.activation(out=QTc[:].rearrange("p j t -> p (j t)"), in_=PS[0][:, 0:4 * CL], func=AF.Identity),
                         R=["ps0"], W=["QTc"])
                    for jt in range(2):
                        for j in range(4):
                            ix = j * 2 + jt
                            g.op("pe", lambda e, j=j, jt=jt, ix=ix: e.matmul(PS[(6, 1)[ix // 4]][:, (ix % 4) * 128:(ix % 4 + 1) * 128],
                                                                            kC[:, jt, j * 128:(j + 1) * 128], ident[:], start=True, stop=True),
                                 R=["kC", "ident"], W=[f"ps{(6, 1)[ix // 4]}"])
                    for hb in range(2):
                        g.op("act", lambda e, hb=hb: e.activation(
                            out=KTc[:, hb * 2:(hb + 1) * 2, :, :].rearrange("p j c t -> p (j c t)"), in_=PS[(6, 1)[hb]][:, :], func=AF.Identity),
                            R=[f"ps{(6, 1)[hb]}"], W=[("KTc", hb)])
                    for h in range(8):
                        g.op("dve", lambda e, h=h: e.tensor_tensor(out=QTcm[:, h, :], in0=QTc[:, h // 2, :], in1=HM[:, h, 0:CL],
                                                                   op=ALU.mult), R=["QTc"] + HMk, W=[("QTcm", h)])
                    for jt in range(2):
                        for h in range(8):
                            c0 = (jt * 8 + h) * CL
                            g.op("pe", lambda e, h=h, jt=jt, c0=c0: e.matmul(
                                PS[2][:, c0:c0 + CL], KTc[:, h // 2, jt, :], QTcm[:, h, :],
                                start=True, stop=True), R=[("KTc", 0), ("KTc", 1), ("QTcm", h)], W=["ps2"])
                    g.op("dve", lambda e: e.tensor_tensor(out=attc[:], in0=PS[2][:, :].rearrange("p (c h t) -> p c h t", c=2, h=8),
                                                          in1=DCT[:], op=ALU.mult), R=["ps2", "DCT"], W=["attc"])
                    for h in range(8):
                        bk = 4 + h // 4
                        sl = slice((h % 4) * 128, (h % 4 + 1) * 128)
                        for jt in range(2):
                            g.op("pe", lambda e, h=h, jt=jt, bk=bk, sl=sl: e.matmul(
                                PS[bk][:CL, sl], attc[:, jt, h, :], vC[:, jt, h * 128:(h + 1) * 128], start=(jt == 0), stop=(jt == 1)),
                                R=["attc", "vC"], W=[f"ps{bk}"])
                    finish(CL, Tc, NT)
                    g.fence()
                    if STOP <= 6:
                        return

            with ExitStack() as S:
                KT = [sb(S, f"KT{i}", [128, NK], BF16) for i in range(2)]
                VT = [sb(S, f"VT{i}", [128, NKT, 128], BF16) for i in range(2)]
                QT = [sb(S, f"QT{i}", [128, TQ], BF16) for i in range(2)]
                Pb = [sb(S, f"Pb{i}", [128, QB], BF16) for i in range(3)]
                rc = [sb(S, f"rc{i}", [128, QB], F32) for i in range(2)]
                oo = [sb(S, f"oo{i}", [128, QB], F32) for i in range(2)]
                od = sb(S, "od", [128, QB], F32)
                osq2 = sb(S, "osq2", [128, QB], BF16)
                sd2 = sb(S, "sd2", [128, QB], F32); rs2 = sb(S, "rs2", [128, QB], F32)
                gt = [sb(S, f"gt{i}", [128, QB], BF16) for i in range(2)]
                bo = [sb(S, f"bo{i}", [128, QB], BF16) for i in range(2)]
                dl = sb(S, "dl", [128, 256], F32); dl2 = sb(S, "dl2", [128, 2], F32)
                nlam = sb(S, "nlam", [128, 1], F32)
                dsg = sb(S, "dsg", [128, 1], F32)
                dlp = sb(S, "dlp", [128, 2, 64], F32)
                tload([(dl, dlam_in.ap()[l:l + 1, :].partition_broadcast(128))], force_pool=True)
                tload([(dsg, dsg_in.ap()[l])])
                dlv = dl[:].rearrange("p (a b d) -> p a b d", a=2, b=2)
                g.op("dve", lambda e: e.tensor_tensor(out=dlp[:], in0=dlv[:, :, 0, :], in1=dlv[:, :, 1, :], op=ALU.mult), R=["dl"], W=["dlp"])
                g.op("dve", lambda e: e.tensor_reduce(out=dl2[:], in_=dlp[:], axis=AX.X, op=ALU.add), R=["dlp"], W=["dl2"])
                g.op("act", lambda e: e.activation(out=dl2[:], in_=dl2[:], func=AF.Exp), R=["dl2"], W=["dl2"])
                g.op("dve", lambda e: e.tensor_tensor(out=nlam[:], in0=dl2[:, 1:2], in1=dl2[:, 0:1], op=ALU.subtract), R=["dl2"], W=["nlam"])
                g.op("dve", lambda e: e.tensor_scalar(out=nlam[:], in0=nlam[:], scalar1=-lam_init, scalar2=None, op0=ALU.add),
                     R=["nlam"], W=["nlam"])
                g.op("dve", lambda e: e.tensor_scalar(out=dsg[:], in0=dsg[:], scalar1=1.0 - lam_init, scalar2=None, op0=ALU.mult),
                     R=["dsg"], W=["dsg"])

                KTav = KTa[par].ap().rearrange("(r c) t -> c r t", r=NCORE)
                groups = []
                for h in range(8):
                    groups.append(dict(kind="d", krow=2 * h * 64, vcol=h * 128,
                                       passes=[dict(qrow=2 * h * 64, qdst=QDT, lo=0, hi=64, sc=64 ** -0.5),
                                               dict(qrow=None, lo=64, hi=128, sc=64 ** -0.5)], h=h))
                for kh in range(2):
                    groups.append(dict(kind="g", krow=1024 + kh * 128, vcol=1024 + kh * 128,
                                       passes=[dict(qrow=(kh * 4 + a) * 128, qdst=QGT, lo=0, hi=128, sc=128 ** -0.5, h=kh * 4 + a)
                                               for a in range(4)]))
                qi = 0; gci = 0; fi = 0; pbi = 0
                for gi, gr in enumerate(groups):
                    K_ = KT[gi % 2]; V_ = VT[gi % 2]
                    kr = gr["krow"]; vc = gr["vcol"]

                    def ldk(e):
                        ins = [e.dma_start(out=K_[:, 0:NCORE * Tc].rearrange("p (r t) -> p r t", r=NCORE),
                                           in_=KTav[kr:kr + 128, :, 0:Tc]),
                               e.dma_start(out=K_[:, NCORE * Tc:NK].rearrange("p (r t) -> p r t", r=NCORE),
                                           in_=KTav[kr:kr + 128, :, Tc:TQ])]
                        return ins

                    def ldv(e):
                        ins = []
                        for r in range(NCORE):
                            ins.append(e.dma_start(out=V_[:, r * NT:(r + 1) * NT, :],
                                                   in_=Va[par].ap()[r * TQ:r * TQ + Tc, vc:vc + 128].rearrange("(t p) e -> p t e", p=128)))
                            ins.append(e.dma_start(out=V_[32 * (r % 4):32 * (r % 4) + 32, NCORE * NT + r // 4, :],
                                                   in_=Va[par].ap()[r * TQ + Tc:r * TQ + TQ, vc:vc + 128]))
                        return ins
                    g.op("sp", ldk, R=["KTa"], W=[K_.name], dma=K_.name)
                    g.op("sp", ldv, R=["Va"], W=[V_.name], dma=V_.name)
                    Q_ = None
                    if gr["kind"] == "d":
                        work = [(pi_, bi) for bi in range(len(blocks)) for pi_ in range(2)]
                    else:
                        work = [(pi_, bi) for pi_ in range(4) for bi in range(len(blocks))]
                    qloaded = set()
                    for (pi_, bi) in work:
                        ps_ = gr["passes"][pi_]
                        if ps_["qrow"] is not None and pi_ not in qloaded:
                            qloaded.add(pi_)
                            Q_ = QT[qi % 2]; qi += 1
                            qr = ps_["qrow"]; qd = ps_["qdst"]
                            g.op("sp", lambda e: e.dma_start(out=Q_[:], in_=qd.ap()[qr:qr + 128, :]), W=[Q_.name], dma=Q_.name)
                        lo, hi, sc = ps_["lo"], ps_["hi"], ps_["sc"]
                        for (t0, nb) in [blocks[bi]]:
                            kts = list(range(NKT)) if t0 < Tc else [NKT - 2, NKT - 1]
                            pr_ = gci % 2; gci += 1
                            bO = 2 + 2 * pr_; bR = 3 + 2 * pr_
                            kO = f"ps{bO}"; kR = f"ps{bR}"

                            def smm(i):
                                kt = kts[i]
                                g.op("pe", lambda e: e.matmul(PS[i % 2][:, :nb], K_[lo:hi, kt * 128:(kt + 1) * 128], Q_[lo:hi, t0:t0 + nb],
                                                              start=True, stop=True), R=[K_.name, Q_.name], W=[f"ps{i % 2}"])
                            smm(0)
                            for i, kt in enumerate(kts):
                                P_ = Pb[pbi % 3]; pbi += 1
                                g.op("act", lambda e: e.activation(out=P_[:, :nb], in_=PS[i % 2][:, :nb], func=AF.Exp, scale=sc),
                                     R=[f"ps{i % 2}"], W=[P_.name])
                                if i + 1 < len(kts):
                                    smm(i + 1)
                                g.op("pe", lambda e: e.matmul(PS[bO][:, :nb], V_[:, kt, :], P_[:, :nb], start=(i == 0), stop=(i == len(kts) - 1)),
                                     R=[V_.name, P_.name], W=[kO])
                                g.op("pe", lambda e: e.matmul(PS[bR][:, :nb], ones[:], P_[:, :nb], start=(i == 0), stop=(i == len(kts) - 1)),
                                     R=["ones", P_.name], W=[kR])
                            c_ = pi_ % 2 if gr["kind"] == "d" else 0
                            g.op("dve", lambda e: e.reciprocal(out=rc[c_][:, :nb], in_=PS[bR][:, :nb]), R=[kR], W=[rc[c_].name])
                            g.op("dve", lambda e: e.tensor_tensor(out=oo[c_][:, :nb], in0=PS[bO][:, :nb], in1=rc[c_][:, :nb], op=ALU.mult),
                                 R=[kO, rc[c_].name], W=[oo[c_].name])
                            if gr["kind"] == "d" and pi_ == 0:
                                continue
                            g_ = gt[fi % 2]; b_ = bo[fi % 2]; fi += 1
                            if gr["kind"] == "d":
                                hh = gr["h"]; grow = 1024 + hh * 128
                                g.op("sp", lambda e: e.dma_start(out=g_[:, :nb], in_=GSIL.ap()[grow:grow + 128, t0:t0 + nb]),
                                     W=[g_.name], dma=g_.name)
                                g.op("dve", lambda e: e.scalar_tensor_tensor(
                                    out=od[:, :nb], in0=oo[1][:, :nb], scalar=nlam[:, 0:1], in1=oo[0][:, :nb], op0=ALU.mult, op1=ALU.add),
                                    R=[oo[0].name, oo[1].name, "nlam"], W=["od"])
                                g.op("act", lambda e: e.activation(out=osq2[:, :nb], in_=od[:, :nb], func=AF.Square), R=["od"], W=["osq2"])
                                g.op("pe", lambda e: e.matmul(PS[6][:, :nb], ones[:], osq2[:, :nb], start=True, stop=True),
                                     R=["ones", "osq2"], W=["ps6"])
                                g.op("act", lambda e: e.activation(out=sd2[:, :nb], in_=PS[6][:, :nb], func=AF.Sqrt, bias=epsc[:],
                                                                   scale=1.0 / 128), R=["ps6", "epsc"], W=["sd2"])
                                g.op("dve", lambda e: e.reciprocal(out=rs2[:, :nb], in_=sd2[:, :nb]), R=["sd2"], W=["rs2"])
                                g.op("dve", lambda e: e.scalar_tensor_tensor(
                                    out=od[:, :nb], in0=od[:, :nb], scalar=dsg[:, 0:1], in1=rs2[:, :nb], op0=ALU.mult, op1=ALU.mult),
                                    R=["od", "dsg", "rs2"], W=["od"])
                                g.op("pool", lambda e: e.tensor_tensor(out=b_[:, :nb], in0=od[:, :nb], in1=g_[:, :nb], op=ALU.mult),
                                     R=["od", g_.name], W=[b_.name])
                            else:
                                hh = ps_["h"]; grow = 2048 + hh * 128
                                g.op("sp", lambda e: e.dma_start(out=g_[:, :nb], in_=GSIL.ap()[grow:grow + 128, t0:t0 + nb]),
                                     W=[g_.name], dma=g_.name)
                                g.op("pool", lambda e: e.tensor_tensor(out=b_[:, :nb], in0=oo[0][:, :nb], in1=g_[:, :nb], op=ALU.mult),
                                     R=[oo[0].name, g_.name], W=[b_.name])
                            g.op("pool", lambda e: e.dma_start(out=BRT.ap()[grow:grow + 128, t0:t0 + nb], in_=b_[:, :nb]),
                                 R=[b_.name], dma="st_" + b_.name)
                g.fence()
                if STOP <= 7:
                    return

            with ExitStack() as S:
                WB = sb(S, "WB", [128, 3, 8, D], BF16)
                brT = [sb(S, f"brT{i}", [128, 3, 8, QB], BF16) for i in range(2)]
                sg = [sb(S, f"sg{i}", [128, 3, QB], BF16) for i in range(2)]
                ty = [sb(S, f"ty{i}", [128, QB], F32) for i in range(3)]
                mT = [sb(S, f"mT{i}", [128, KC, QB], BF16) for i in range(2)]
                g.op("sp", lambda e: [e.dma_start(out=WB[:, n_], in_=WBR[l].ap()[n_ * 1024:(n_ + 1) * 1024, :].rearrange("(k p) d -> p k d", p=128))
                                      for n_ in range(3)], W=["WB"], dma="WB")
                BRv = BRT.ap().rearrange("(n k p) t -> p n k t", n=3, p=128)
                SGv = GSIG.ap().rearrange("(n j p) t -> p n j t", n=3, p=128)
                MTv = MT.ap().rearrange("(j p) t -> p j t", p=128)
                si = 0
                for bi, (t0, nb) in enumerate(blocks):
                    b_ = brT[bi % 2]; m_ = mT[bi % 2]
                    g.op("sp", lambda e: [e.dma_start(out=b_[:, n_, :, :nb], in_=BRv[:, n_, :, t0:t0 + nb]) for n_ in range(3)],
                         R=["BRT"], W=[b_.name], dma=b_.name)
                    for j in range(KC):
                        s_ = sg[si % 2]; si += 1
                        g.op("sp", lambda e: e.dma_start(out=s_[:, :, :nb], in_=SGv[:, :, j, t0:t0 + nb]), R=["GSIG"], W=[s_.name], dma=s_.name)
                        for n_ in range(3):
                            bk = n_ + 3 * (j % 2)
                            for k in range(8):
                                g.op("pe", lambda e, k=k, bk=bk, n_=n_: e.matmul(PS[bk][:, :nb], WB[:, n_, k, j * 128:(j + 1) * 128],
                                                                               b_[:, n_, k, :nb], start=(k == 0), stop=(k == 7)),
                                     R=["WB", b_.name], W=[f"ps{bk}"])
                            g.op("dve", lambda e, bk=bk, n_=n_: e.tensor_tensor(out=ty[n_][:, :nb], in0=PS[bk][:, :nb], in1=s_[:, n_, :nb],
                                                                               op=ALU.mult), R=[f"ps{bk}", s_.name], W=[ty[n_].name])
                        g.op("pool", lambda e: e.tensor_tensor(out=ty[0][:, :nb], in0=ty[0][:, :nb], in1=ty[1][:, :nb], op=ALU.add),
                             R=[ty[0].name, ty[1].name], W=[ty[0].name])
                        g.op("pool", lambda e, j=j: e.tensor_tensor(out=m_[:, j, :nb], in0=ty[0][:, :nb], in1=ty[2][:, :nb], op=ALU.add),
                             R=[ty[0].name, ty[2].name], W=[(m_.name, j)])
                    g.op("pool", lambda e: [e.dma_start(out=MT.ap()[j * 128:(j + 1) * 128, t0:t0 + nb], in_=m_[:, j, :nb])
                                            for j in range(KC)],
                         R=[(m_.name, j) for j in range(KC)], dma="st_" + m_.name)
                g.fence()
                if STOP <= 8:
                    return

            with ExitStack() as S:
                WO = sb(S, "WO", [128, KC, D], BF16)
                mT2 = [sb(S, f"mU{i}", [128, KC, QB], BF16) for i in range(2)]
                xr = [sb(S, f"xr{i}", [128, KC, QB], F32) for i in range(2)]
                g.op("sp", lambda e: e.dma_start(out=WO[:], in_=WOUT[l].ap().rearrange("(k p) d -> p k d", p=128)), W=["WO"], dma="WO")
                MTv = MT.ap().rearrange("(j p) t -> p j t", p=128)
                for bi, (t0, nb) in enumerate(blocks):
                    cx = 1 if t0 >= Tc else 0
                    m_ = mT2[bi % 2]; x_ = xr[bi % 2]
                    g.op("sp", lambda e: e.dma_start(out=m_[:, :, :nb], in_=MTv[:, :, t0:t0 + nb]), R=["MT"], W=[m_.name], dma=m_.name)
                    g.op("sp", lambda e: e.dma_start(out=x_[:, :, :nb], in_=XTv[:, :, t0:t0 + nb]), R=["XT"], W=[x_.name], dma=x_.name)
                    for j2 in range(KC):
                        bk = j2 % 4
                        for j in range(KC):
                            g.op("pe", lambda e, j=j, j2=j2, bk=bk: e.matmul(PS[bk][:, :nb], WO[:, j, j2 * 128:(j2 + 1) * 128], m_[:, j, :nb],
                                                                           start=(j == 0), stop=(j == KC - 1)), R=["WO", m_.name], W=[f"ps{bk}"])
                        g.op("dve", lambda e, j2=j2, bk=bk: e.scalar_tensor_tensor(
                            out=x_[:, j2, :nb], in0=PS[bk][:, :nb], scalar=MOD[:, 32 + j2, cx:cx + 1], in1=x_[:, j2, :nb],
                            op0=ALU.mult, op1=ALU.add), R=[f"ps{bk}", "MOD", (x_.name, j2), x_.name], W=[(x_.name, j2)])
                    g.op("pool", lambda e: [e.dma_start(out=XT.ap()[j, :, t0:t0 + nb], in_=x_[:, j, :nb]) for j in range(KC)],
                         R=[(x_.name, j) for j in range(KC)] + [x_.name], dma="st_" + x_.name)
                g.fence()


    for l in range(DEPTH):
        if STOP >= 1:
            layer(l)

    with ExitStack() as S:
        xt = [sb(S, f"fx{i}", [128, KC, QB], F32) for i in range(2)]
        sq = [sb(S, f"fs{i}", [128, KC, QB], BF16) for i in range(2)]
        sd = sb(S, "fsd", [128, QB], F32); rs = sb(S, "frs", [128, QB], F32)
        yv = yT_out.ap().rearrange("(k p) t -> p k t", p=128)
        for bi, (t0, nb) in enumerate(blocks[:NB]):
            x_ = xt[bi % 2]; s_ = sq[bi % 2]
            g.op("sp", lambda e: e.dma_start(out=x_[:, :, :nb], in_=XTv[:, :, t0:t0 + nb]), W=[x_.name], dma=x_.name)
            g.op("act", lambda e: e.activation(out=s_[:, :, :nb], in_=x_[:, :, :nb], func=AF.Square), R=[x_.name], W=[s_.name])
            for k in range(KC):
                g.op("pe", lambda e, k=k: e.matmul(PS[1][:, :nb], ones[:], s_[:, k, :nb], start=(k == 0), stop=(k == KC - 1)),
                     R=[s_.name, "ones"], W=["ps1"])
            g.op("act", lambda e: e.activation(out=sd[:, :nb], in_=PS[1][:, :nb], func=AF.Sqrt, bias=epsc[:], scale=1.0 / D),
                 R=["ps1", "epsc"], W=["sd"])
            g.op("dve", lambda e: e.reciprocal(out=rs[:, :nb], in_=sd[:, :nb]), R=["sd"], W=["rs"])
            g.op("dve", lambda e: e.tensor_tensor(out=x_[:, :, :nb], in0=x_[:, :, :nb],
                                                  in1=rs[:, :nb].unsqueeze(1).broadcast_to([128, KC, nb]), op=ALU.mult),
                 R=[x_.name, "rs"], W=[x_.name])
            g.op("pool", lambda e: e.tensor_tensor(out=x_[:, :, :nb], in0=x_[:, :, :nb],
                                                   in1=fng[:].unsqueeze(2).broadcast_to([128, KC, nb]), op=ALU.mult),
                 R=[x_.name, "fng"], W=[x_.name])
            g.op("pool", lambda e: [e.dma_start(out=yT_out.ap()[j * 128:(j + 1) * 128, t0:t0 + nb], in_=x_[:, j, :nb])
                                    for j in range(KC)], R=[x_.name], dma="st_" + x_.name)
        g.fence()
    stack.close()
    return nc


def _tables(SEQ, c):
    Tc = SEQ // NCORE
    NT = Tc // 128
    TQ = Tc + CL
    TT = NT + 1
    f32 = np.float32
    pos = np.arange(c * Tc, (c + 1) * Tc)
    rows = (pos // GRID_W).astype(f32)
    cols = (pos % GRID_W).astype(f32)

    def axial(hd, rep):
        nf = hd // 4
        fr = (np.float32(10000.0) ** (-np.arange(nf, dtype=f32) / f32(nf))).astype(f32)
        ang = np.concatenate([rows[:, None] * fr, cols[:, None] * fr], axis=-1).astype(f32)
        cs = np.ones((hd // 2, TQ), f32); sn = np.zeros((hd // 2, TQ), f32)
        cs[:, :Tc] = np.cos(ang).T; sn[:, :Tc] = np.sin(ang).T
        return np.ascontiguousarray(np.tile(cs, (rep, 1))), np.ascontiguousarray(np.tile(sn, (rep, 1)))
    cosD, sinD = axial(64, 4)
    cosG, sinG = axial(128, 2)
    fr = (f32(1.0) / (f32(10000.0) ** np.linspace(0.0, 1.0, 32, dtype=f32))).astype(f32)
    ang = (pos.astype(f32)[:, None] * fr).astype(f32)
    cR = np.ones((TT * 128, 32), f32); sR = np.zeros((TT * 128, 32), f32)
    cR[:Tc] = np.cos(ang); sR[:Tc] = np.sin(ang)
    cR = np.ascontiguousarray(cR.reshape(TT, 128, 32).transpose(1, 0, 2)); sR = np.ascontiguousarray(sR.reshape(TT, 128, 32).transpose(1, 0, 2))
    j = np.arange(128)[:, None]; i = np.arange(128)[None, :]
    d = {}
    d.update(cosD=cosD, sinD=sinD, cosG=cosG, sinG=sinG, cosR=cR, sinR=sR,
             cosK=(cR * f32(0.125)).astype(f32), sinK=(sR * f32(0.125)).astype(f32))
    d["ident"] = np.eye(128, dtype=f32)
    bd = np.zeros((128, 128), f32); bd[:64, :64] = 1; bd[64:, 64:] = 1
    d["bdones"] = bd
    d["relf"] = np.maximum(i - j, 0).astype(f32); d["mf"] = (j <= i).astype(f32)
    d["relb"] = np.maximum(j - i, 0).astype(f32); d["mb"] = (j > i).astype(f32)
    ip1 = np.zeros((128, 128), f32); ip1[:64] = np.arange(128) + 1; ip1[64:] = 128 - np.arange(128)
    d["ip1"] = ip1
    jx = np.zeros((128, 16), f32); jx[:, :8] = 127 - np.arange(128)[:, None]; jx[:, 8:] = np.arange(128)[:, None]
    d["jx"] = jx
    ne = np.zeros((128, NT), f32); ne[:64] = 128.0 * np.arange(NT); ne[64:] = 128.0 * (NT - 1 - np.arange(NT))
    d["nexp"] = ne
    ce = np.zeros((128, 9), f32); cm = np.zeros((128, 9), f32)
    for c2 in range(NCORE):
        if c2 < c:
            ce[:64, c2] = Tc * (c - 1 - c2); cm[:64, c2] = 1
        if c2 > c:
            ce[64:, c2] = Tc * (c2 - c - 1); cm[64:, c2] = 1
    ce[:64, 8] = Tc * c; ce[64:, 8] = Tc * (NCORE - 1 - c); cm[:, 8] = 1
    d["ce"] = ce; d["cm"] = cm
    p = (CL * c + np.arange(CL))[None, None, :]
    jj = (np.arange(2)[None, :, None] * 128 + np.arange(128)[:, None, None])
    d["crelf"] = np.maximum(p - jj, 0).astype(f32); d["cmf"] = (jj <= p).astype(f32)
    d["crelb"] = np.maximum(jj - p, 0).astype(f32); d["cmb"] = (jj > p).astype(f32)
    jxc = np.zeros((128, 2, 16), f32)
    jg = np.arange(2)[None, :] * 128 + np.arange(128)[:, None]
    jxc[:, :, :8] = (255 - jg)[:, :, None]; jxc[:, :, 8:] = jg[:, :, None]
    d["jxc"] = jxc
    return {k: np.ascontiguousarray(v, dtype=f32) for k, v in d.items()}


_PROG = {}


def make_in_maps(x, c, ctx, c_ctx, norm_gain, w_ada, b_ada, w_in, ret_log_rate, diff_lambda, diff_subln_gain,
                 gqa_q_gain, gqa_k_gain, w_branch, w_out, final_norm_gain):
    f32 = np.float32
    x = np.asarray(x, f32); ctx = np.asarray(ctx, f32)
    SEQ = x.shape[1]; DEPTH = w_in.shape[0]
    Tc = SEQ // NCORE
    perm = _w_in_perm()
    w_in_p = np.asarray(w_in, f32)[:, :, perm]
    csc = np.stack([np.asarray(c, f32).reshape(KC, 128).T, np.asarray(c_ctx, f32).reshape(KC, 128).T], axis=-1)
    ng = np.asarray(norm_gain, f32).reshape(DEPTH, KC, 128).transpose(0, 2, 1)
    bada = np.asarray(b_ada, f32).reshape(DEPTH, 48, 128).transpose(0, 2, 1)
    fng = np.asarray(final_norm_gain, f32).reshape(KC, 128).T
    qg = np.asarray(gqa_q_gain, f32); kg = np.asarray(gqa_k_gain, f32)
    gqg = np.stack([np.concatenate([qg[:, :64], qg[:, :64]], 1), np.concatenate([qg[:, 64:], qg[:, 64:]], 1),
                    np.concatenate([kg[:, :64], kg[:, :64]], 1), np.concatenate([kg[:, 64:], kg[:, 64:]], 1)], axis=-1)
    wbr = np.asarray(w_branch, f32).reshape(DEPTH, 3072, D)
    common = dict(
        csc=csc, ng=ng, bada=bada, fng=fng, rlr=np.asarray(ret_log_rate, f32).reshape(DEPTH, 16),
        dlam=np.asarray(diff_lambda, f32).reshape(DEPTH, 256), dsg=np.asarray(diff_subln_gain, f32).reshape(DEPTH, 128, 1),
        gqg=gqg)
    common = {k: np.ascontiguousarray(v, dtype=f32) for k, v in common.items()}
    in_maps = []
    for ci in range(NCORE):
        m = dict(common)
        m["xT"] = np.ascontiguousarray(x[0, ci * Tc:(ci + 1) * Tc, :].T)
        m["ctxT"] = np.ascontiguousarray(ctx[0, ci * CL:(ci + 1) * CL, :].T)
        r = D // NCORE
        m["w_in_s"] = np.ascontiguousarray(w_in_p[:, ci * r:(ci + 1) * r, :])
        m["w_ada_s"] = np.ascontiguousarray(np.asarray(w_ada, f32)[:, ci * r:(ci + 1) * r, :])
        m["w_out_s"] = np.ascontiguousarray(np.asarray(w_out, f32)[:, ci * r:(ci + 1) * r, :])
        rb = 3072 // NCORE
        m["w_br_s"] = np.ascontiguousarray(wbr[:, ci * rb:(ci + 1) * rb, :])
        m.update(_tables(SEQ, ci))
        in_maps.append(m)
    return in_maps, SEQ, DEPTH


def kernel(**inputs):
    f32 = np.float32
    in_maps, SEQ, DEPTH = make_in_maps(**inputs)
    Tc = SEQ // NCORE
    key = (SEQ, DEPTH)
    if key not in _PROG:
        _PROG[key] = build_program(SEQ, DEPTH)
    nc = _PROG[key]
    res = run_bass_kernel_spmd(nc, in_maps, core_ids=list(range(NCORE)))
    out = np.empty((1, SEQ, D), f32)
    for ci in range(NCORE):
        out[0, ci * Tc:(ci + 1) * Tc, :] = np.asarray(res.results[ci]["yT"], f32).T
    return out
```
